# Optimizing a Trainium2 kernel written in Bass

```python
import math
import jax, jax.numpy as jnp
from jax import lax
import numpy as np

D_MODEL = 1024
BATCH = 2
SEQ = 8192
DEPTH = 4
DEC_BATCH = 16
DEC_SEQ = 32
PAST_LEN = 1024

CHUNK = 64
Q_BLOCK = 128
H_A = 8
DK_A = 128
DV_A = 128
QK_A = H_A * DK_A
W_A = H_A * DV_A
CONV_W = 4
CONV_CH = 2 * QK_A + W_A
H_B = 8
D_QK = 64
DV_B = 2 * D_QK
QK_B = H_B * 2 * D_QK
W_B = H_B * DV_B
ROPE_THETA = 10000.0
EPS = 1e-6
IN_SIZES = (CONV_CH, W_A, H_A, H_A, QK_B, QK_B, W_B, W_B, D_MODEL, D_MODEL)
IN_COLS = CONV_CH + W_A + 2 * H_A + 2 * QK_B + 2 * W_B + 2 * D_MODEL

kernel_name = 'hybrid_gdn_diffattn_stream'


def rmsnorm(x, w):
    xf = x.astype(jnp.float32)
    xf = xf * lax.rsqrt(jnp.mean(xf * xf, axis=-1, keepdims=True) + EPS)
    return xf.astype(x.dtype) * w


def ada_norm(x, c, norm_w, w_ada, b_ada):
    mod = (c @ w_ada + b_ada)[:, None, :]
    shift, scale, gate = jnp.split(mod, 3, axis=-1)
    return rmsnorm(x, norm_w) * (1 + scale) + shift, gate


def split_cols(p):
    outs, off = [], 0
    for s in IN_SIZES:
        outs.append(p[..., off:off + s])
        off += s
    return outs


def causal_conv_silu(u, buf, w):
    T = u.shape[1]
    full = jnp.concatenate([buf.astype(u.dtype), u], axis=1)
    out = full[:, 0:T] * w[0]
    for j in range(1, CONV_W):
        out = out + full[:, j:j + T] * w[j]
    return jax.nn.silu(out), full[:, T:]


def l2norm(x):
    return x * lax.rsqrt(jnp.sum(x * x, axis=-1, keepdims=True) + EPS)


def gated_delta_rule(q, k, v, g, beta, s0):
    B, T, H, _ = q.shape
    L = min(CHUNK, T)
    n = T // L
    f32 = jnp.float32
    q = l2norm(q.astype(f32)) * (DK_A ** -0.5)
    k = l2norm(k.astype(f32))

    def to_chunks(t):
        t = t.reshape((B, n, L, H) + t.shape[3:])
        return jnp.moveaxis(t, (1, 3), (0, 2))

    qc, kc, vc = to_chunks(q), to_chunks(k), to_chunks(v.astype(f32))
    gc, bc = to_chunks(g.astype(f32)), to_chunks(beta.astype(f32))
    G = jnp.cumsum(gc, axis=-1)
    diff = G[..., :, None] - G[..., None, :]
    idx = jnp.arange(L)
    incl = idx[:, None] >= idx[None, :]
    strict = idx[:, None] > idx[None, :]
    dec_incl = jnp.exp(jnp.where(incl, diff, -jnp.inf))
    dec_strict = jnp.where(strict, dec_incl, 0.0)
    kk = jnp.einsum('nbhid,nbhjd->nbhij', kc, kc)
    a_mat = jnp.eye(L, dtype=f32) + bc[..., :, None] * kk * dec_strict
    rhs = jnp.concatenate([bc[..., None] * vc, (bc * jnp.exp(G))[..., None] * kc], axis=-1)
    sol = lax.linalg.triangular_solve(a_mat, rhs, left_side=True, lower=True, unit_diagonal=True)
    v_new, k_cum = sol[..., :DV_A], sol[..., DV_A:]
    qk = jnp.einsum('nbhid,nbhjd->nbhij', qc, kc) * dec_incl
    q_dec = qc * jnp.exp(G)[..., None]
    k_dec = kc * jnp.exp(G[..., -1:] - G)[..., None]
    g_last = jnp.exp(G[..., -1])[..., None, None]

    def step(s, xs):
        vn, kcum, qk_c, qd, kd, gl = xs
        w = vn - jnp.einsum('bhid,bhdv->bhiv', kcum, s)
        o = jnp.einsum('bhid,bhdv->bhiv', qd, s) + jnp.einsum('bhij,bhjv->bhiv', qk_c, w)
        s = gl * s + jnp.einsum('bhjd,bhjv->bhdv', kd, w)
        return s, o

    s_final, o = lax.scan(step, s0.astype(f32), (v_new, k_cum, qk, q_dec, k_dec, g_last))
    o = jnp.moveaxis(o, (0, 2), (1, 3)).reshape(B, T, H, DV_A)
    return o, s_final


def rope(x, pos):
    half = x.shape[-1] // 2
    inv = ROPE_THETA ** (-jnp.arange(half, dtype=jnp.float32) / half)
    ang = pos.astype(jnp.float32)[:, None] * inv[None, :]
    cos = jnp.cos(ang)[None, :, None, None, :].astype(x.dtype)
    sin = jnp.sin(ang)[None, :, None, None, :].astype(x.dtype)
    x1, x2 = x[..., :half], x[..., half:]
    return jnp.concatenate([x1 * cos - x2 * sin, x1 * sin + x2 * cos], axis=-1)


def diff_attn_core(q, k, v, lam, mask):
    s = jnp.einsum('bhqmd,bhkmd->bhmqk', q, k).astype(jnp.float32) * (D_QK ** -0.5)
    if mask is not None:
        s = jnp.where(mask, s, -jnp.inf)
    a = jax.nn.softmax(s, axis=-1)
    a = a[:, :, 0] - lam * a[:, :, 1]
    return jnp.einsum('bhqk,bhkv->bhqv', a.astype(v.dtype), v)


def diff_attn_prompt(q, k, v, lam):
    B, H, T = q.shape[:3]
    nb = T // Q_BLOCK
    q_blocks = jnp.moveaxis(q.reshape(B, H, nb, Q_BLOCK, 2, D_QK), 2, 0)
    key_chunk = jnp.arange(T) // CHUNK

    def one_block(args):
        qb, i = args
        q_chunk = (i * Q_BLOCK + jnp.arange(Q_BLOCK)) // CHUNK
        mask = key_chunk[None, :] <= q_chunk[:, None]
        return diff_attn_core(qb, k, v, lam, mask)

    o = lax.map(one_block, (q_blocks, jnp.arange(nb)))
    return jnp.moveaxis(o, 0, 2).reshape(B, H, T, DV_B)


def mixer_layer(x, c, pos, conv_buf, ssm0, past_k, past_v, p, lam_init):
    B, T, _ = x.shape
    h, gate = ada_norm(x, c, p['norm_w'], p['w_ada'], p['b_ada'])
    u_a, z_a, b_a, a_a, q_b, k_b, v_b, z_b, g_a, g_b = split_cols(h @ p['w_in'])
    u_a, conv_new = causal_conv_silu(u_a, conv_buf, p['conv_w'])
    q_a = u_a[..., :QK_A].reshape(B, T, H_A, DK_A)
    k_a = u_a[..., QK_A:2 * QK_A].reshape(B, T, H_A, DK_A)
    v_a = u_a[..., 2 * QK_A:].reshape(B, T, H_A, DV_A)
    beta = jax.nn.sigmoid(b_a)
    g = -jnp.exp(p['a_log']) * jax.nn.softplus(a_a + p['dt_bias'])
    o_a, ssm_new = gated_delta_rule(q_a, k_a, v_a, g, beta, ssm0)
    o_a = rmsnorm(o_a.astype(x.dtype), p['gdn_norm_w']) * jax.nn.silu(z_a.reshape(B, T, H_A, DV_A))
    o_a = o_a.reshape(B, T, W_A)
    q_b = rope(q_b.reshape(B, T, H_B, 2, D_QK), pos).transpose(0, 2, 1, 3, 4)
    k_b = rope(k_b.reshape(B, T, H_B, 2, D_QK), pos).transpose(0, 2, 1, 3, 4)
    v_b = v_b.reshape(B, T, H_B, DV_B).transpose(0, 2, 1, 3)
    lq = p['lam_qk'].astype(jnp.float32)
    lam = jnp.exp(jnp.sum(lq[0] * lq[1])) - jnp.exp(jnp.sum(lq[2] * lq[3])) + lam_init
    if past_k is None:
        o_b = diff_attn_prompt(q_b, k_b, v_b, lam)
    else:
        P = past_k.shape[2]
        k_all = jnp.concatenate([past_k.reshape(B, H_B, P, 2, D_QK).astype(k_b.dtype), k_b], axis=2)
        v_all = jnp.concatenate([past_v.astype(v_b.dtype), v_b], axis=2)
        o_b = diff_attn_core(q_b, k_all, v_all, lam, None)
    o_b = rmsnorm(o_b.transpose(0, 2, 1, 3), p['diff_norm_w']) * (1.0 - lam_init)
    o_b = (o_b * jax.nn.silu(z_b.reshape(B, T, H_B, DV_B))).reshape(B, T, W_B)
    merged = jax.nn.sigmoid(g_a) * (o_a @ p['w_proj_a']) + jax.nn.sigmoid(g_b) * (o_b @ p['w_proj_b'])
    x = x + gate * (merged @ p['w_out'])
    new_k = k_b.reshape(B, H_B, T, 2 * D_QK)
    return x, new_k, v_b, ssm_new, conv_new


def setup_inputs(seed: int = 0) -> dict:
    key = jax.random.key(seed)
    ks = jax.random.split(key, 24)
    f32 = jnp.float32

    def nrm(k, shape, s):
        return s * jax.random.normal(k, shape, f32)

    dt = jnp.exp(jax.random.uniform(ks[13], (DEPTH, H_A), f32, math.log(1e-3), math.log(1e-1)))
    return {
        'x_prompt': nrm(ks[0], (BATCH, SEQ, D_MODEL), 1.0),
        'x_sample': nrm(ks[1], (DEC_BATCH, DEC_SEQ, D_MODEL), 1.0),
        'c_prompt': nrm(ks[2], (BATCH, D_MODEL), 1.0),
        'c_sample': nrm(ks[3], (DEC_BATCH, D_MODEL), 1.0),
        'cache_k': nrm(ks[4], (DEPTH, DEC_BATCH, H_B, PAST_LEN, 2 * D_QK), 1.0),
        'cache_v': nrm(ks[5], (DEPTH, DEC_BATCH, H_B, PAST_LEN, DV_B), 1.0),
        'state_ssm': nrm(ks[6], (DEPTH, DEC_BATCH, H_A, DK_A, DV_A), 0.1),
        'state_conv': nrm(ks[7], (DEPTH, DEC_BATCH, CONV_W - 1, CONV_CH), 1.0),
        'norm_w': 1.0 + nrm(ks[8], (DEPTH, D_MODEL), 0.02),
        'w_ada': nrm(ks[9], (DEPTH, D_MODEL, 3 * D_MODEL), 0.2 * D_MODEL ** -0.5),
        'b_ada': nrm(ks[10], (DEPTH, 3 * D_MODEL), 0.02),
        'w_in': nrm(ks[11], (DEPTH, D_MODEL, IN_COLS), D_MODEL ** -0.5),
        'conv_w': nrm(ks[12], (DEPTH, CONV_W, CONV_CH), CONV_W ** -0.5),
        'a_log': jnp.log(jax.random.uniform(ks[14], (DEPTH, H_A), f32, 1.0, 16.0)),
        'dt_bias': dt + jnp.log(-jnp.expm1(-dt)),
        'gdn_norm_w': 1.0 + nrm(ks[15], (DEPTH, DV_A), 0.02),
        'lam_qk': nrm(ks[16], (DEPTH, 4, D_QK), 0.1),
        'diff_norm_w': 1.0 + nrm(ks[17], (DEPTH, DV_B), 0.02),
        'w_proj_a': nrm(ks[18], (DEPTH, W_A, D_MODEL), W_A ** -0.5),
        'w_proj_b': nrm(ks[19], (DEPTH, W_B, D_MODEL), W_B ** -0.5),
        'w_out': nrm(ks[20], (DEPTH, D_MODEL, D_MODEL), D_MODEL ** -0.5),
        'final_norm_w': 1.0 + nrm(ks[21], (D_MODEL,), 0.02),
    }


def reference(x_prompt, x_sample, c_prompt, c_sample, cache_k, cache_v, state_ssm, state_conv,
              norm_w, w_ada, b_ada, w_in, conv_w, a_log, dt_bias, gdn_norm_w, lam_qk,
              diff_norm_w, w_proj_a, w_proj_b, w_out, final_norm_w):
    pos_p = jnp.arange(x_prompt.shape[1])
    pos_s = cache_k.shape[3] + jnp.arange(x_sample.shape[1])
    conv0 = jnp.zeros((x_prompt.shape[0], CONV_W - 1, CONV_CH), x_prompt.dtype)
    ssm0 = jnp.zeros((x_prompt.shape[0], H_A, DK_A, DV_A), jnp.float32)
    xp, xs = x_prompt, x_sample
    kp, vp, sp, cp = [], [], [], []
    ksm, vsm, ssm_s, csm = [], [], [], []
    for l in range(DEPTH):
        p = {'norm_w': norm_w[l], 'w_ada': w_ada[l], 'b_ada': b_ada[l], 'w_in': w_in[l],
             'conv_w': conv_w[l], 'a_log': a_log[l], 'dt_bias': dt_bias[l],
             'gdn_norm_w': gdn_norm_w[l], 'lam_qk': lam_qk[l], 'diff_norm_w': diff_norm_w[l],
             'w_proj_a': w_proj_a[l], 'w_proj_b': w_proj_b[l], 'w_out': w_out[l]}
        lam_init = 0.8 - 0.6 * math.exp(-0.3 * l)
        xp, k_, v_, s_, c_ = mixer_layer(xp, c_prompt, pos_p, conv0, ssm0, None, None, p, lam_init)
        kp.append(k_); vp.append(v_); sp.append(s_); cp.append(c_)
        xs, k_, v_, s_, c_ = mixer_layer(xs, c_sample, pos_s, state_conv[l], state_ssm[l],
                                         cache_k[l], cache_v[l], p, lam_init)
        ksm.append(k_); vsm.append(v_); ssm_s.append(s_); csm.append(c_)
    y_prompt = rmsnorm(xp, final_norm_w)
    y_sample = rmsnorm(xs, final_norm_w)
    return (y_prompt, y_sample, jnp.stack(kp), jnp.stack(vp), jnp.stack(sp), jnp.stack(cp),
            jnp.stack(ksm), jnp.stack(vsm), jnp.stack(ssm_s), jnp.stack(csm))
```

```python
import math
from contextlib import ExitStack
import numpy as np
import ml_dtypes
import concourse.bass as bass
import concourse.mybir as mybir
from concourse.bass_utils import run_bass_kernel_spmd

F32 = mybir.dt.float32
BF16 = mybir.dt.bfloat16
AF = mybir.ActivationFunctionType
ALU = mybir.AluOpType
AX = mybir.AxisListType

D = 1024
KC = 8
H = 8
INC = 10256
EPS = 1e-6
C_ZA, C_BA, C_QB, C_KB, C_VB, C_ZB, C_GA, C_GB = 3072, 4096, 4112, 5136, 6160, 7184, 8208, 9232


def _normkey(k):
    if isinstance(k, tuple) and isinstance(k[0], str) and k[0].startswith("ps"):
        return k[0]
    return k


class V:
    __slots__ = ("ap", "keys")

    def __init__(self, ap, keys):
        self.ap = ap
        if isinstance(keys, list):
            ks = keys
        else:
            ks = [keys]
        out = []
        for k in ks:
            k = _normkey(k)
            if k not in out:
                out.append(k)
        self.keys = out


class Prog:
    def __init__(self, nc, stack, ndsem=48):
        self.nc = nc
        self.E = {}
        for name, h in [("pe", nc.tensor), ("dve", nc.vector), ("act", nc.scalar), ("pool", nc.gpsimd), ("sp", nc.sync)]:
            sem = stack.enter_context(nc.semaphore("s_" + name))
            self.E[name] = dict(h=h, sem=sem, cnt=0, known={}, name="s_" + name)
        self.ds = []
        for i in range(ndsem):
            sem = stack.enter_context(nc.semaphore(f"sd{i}"))
            self.ds.append(dict(sem=sem, cnt=0, name=f"sd{i}"))
        self.dnext = 0
        self.lastw = {}
        self.reads = {}
        self.nops = 0

    def _wait(self, eng, ev):
        name, sem, val, src = ev
        if src == eng and eng == "pe":
            return
        e = self.E[eng]
        if e["known"].get(name, 0) >= val:
            return
        e["h"].wait_ge(sem, val)
        e["known"][name] = val

    def _deps(self, outs, ins):
        deps = []
        for v in ins:
            for k in v.keys:
                ev = self.lastw.get(k)
                if isinstance(ev, dict):
                    deps.extend(ev.values())
                elif ev is not None:
                    deps.append(ev)
        for v in outs:
            for k in v.keys:
                ev = self.lastw.get(k)
                if isinstance(ev, dict):
                    deps.extend(ev.values())
                elif ev is not None:
                    deps.append(ev)
                r = self.reads.get(k)
                if r:
                    deps.extend(r.values())
        return deps

    def _reg(self, ev, outs, ins):
        for v in ins:
            for k in v.keys:
                r = self.reads.setdefault(k, {})
                if ev[3] == "dma":
                    r[ev[0]] = ev
                else:
                    r[ev[3]] = ev
        for v in outs:
            for k in v.keys:
                if isinstance(k, str) and k.startswith("D:"):
                    self.lastw.setdefault(k, {})[ev[0]] = ev
                else:
                    self.lastw[k] = ev
                    self.reads[k] = {}

    def barrier_sp(self):
        for slot in self.ds:
            if slot["cnt"] > 0:
                self._wait("sp", (slot["name"], slot["sem"], slot["cnt"], "dma"))

    def op(self, eng, fn, outs, ins, inc=True):
        xin = [v for v in ins if any(isinstance(k, str) and k.startswith("ps") for k in v.keys)]
        if xin:
            ins = [v for v in ins if v not in xin]
            outs = list(outs) + xin
        for ev in self._deps(outs, ins):
            self._wait(eng, ev)
        inst = fn()
        e = self.E[eng]
        if inc:
            e["cnt"] += 1
            inst.then_inc(e["sem"], 1)
            ev = (e["name"], e["sem"], e["cnt"], eng)
        else:
            ev = (e["name"], e["sem"], e["cnt"] + 1, eng)
        self._reg(ev, outs, ins)
        self.nops += 1

    def dma(self, q, out, in_, **kw):
        deps = self._deps([out], [in_])
        slot = self.ds[self.dnext]
        self.dnext = (self.dnext + 1) % len(self.ds)
        if slot["cnt"] > 0:
            self._wait(q, (slot["name"], slot["sem"], slot["cnt"], "dma"))
        for ev in deps:
            self._wait(q, ev)
        inst = self.E[q]["h"].dma_start(out=out.ap, in_=in_.ap, **kw)
        slot["cnt"] += 16
        inst.then_inc(slot["sem"], 16)
        ev = (slot["name"], slot["sem"], slot["cnt"], "dma")
        self._reg(ev, [out], [in_])
        self.nops += 1

    def finish(self):
        for slot in self.ds:
            if slot["cnt"] > 0:
                self._wait("sp", (slot["name"], slot["sem"], slot["cnt"], "dma"))
        for en in ("pe", "dve", "act", "pool"):
            e = self.E[en]
            if e["cnt"] > 0:
                self._wait("sp", (e["name"], e["sem"], e["cnt"], en))


class Ring:
    def __init__(self, nc, stack, name, shape, dtype, n):
        self.ts = [stack.enter_context(nc.sbuf_tensor(f"{name}{i}", shape, dtype)) for i in range(n)]
        self.name = name
        self.i = 0

    def next(self):
        j = self.i % len(self.ts)
        self.i += 1
        return self.ts[j], (self.name, j)


class _Stop(Exception):
    pass


def build(cfg):
    import os
    KSTOP = int(os.environ.get('KSTOP', '99'))

    def stop_at(k):
        if KSTOP == k:
            raise _Stop()
    depth = cfg["depth"]
    TPROMPT = cfg["T"]
    PAST = cfg["P"]
    NSMP = cfg["ns"]
    TS = 32
    lam_inits = [0.8 - 0.6 * math.exp(-0.3 * l) for l in range(depth)]

    nc = bass.Bass("TRN2", target_bir_lowering=False)

    def din(name, shape, dt=F32):
        return nc.dram_tensor(name, list(shape), dt, kind="ExternalInput").ap()

    def dout(name, shape, dt=F32):
        return nc.dram_tensor(name, list(shape), dt, kind="ExternalOutput").ap()

    def dscr(name, shape, dt):
        return nc.dram_tensor(name, list(shape), dt, kind="Internal").ap()

    xp = din("xp", [TPROMPT, D])
    xs = din("xs", [NSMP, TS, D])
    c_all = din("c_all", [1 + NSMP, D])
    cache_k = din("cache_k", [depth, NSMP, H, PAST, 128])
    cache_v = din("cache_v", [depth, NSMP, H, PAST, 128])
    state_ssm = din("state_ssm", [depth, NSMP, H, 128, 128])
    state_conv = din("state_conv", [depth, NSMP, 3, 3072])
    norm_w = din("norm_w", [depth, D])
    w_ada = din("w_ada", [depth, D, 3 * D])
    b_ada = din("b_ada", [depth, 3 * D])
    w_in = din("w_in", [depth, D, INC])
    conv_w = din("conv_w", [depth, 4, 3072])
    a_log = din("a_log", [depth, H])
    dt_bias = din("dt_bias", [depth, H])
    gdn_norm_w = din("gdn_norm_w", [depth, 128])
    lam_qk = din("lam_qk", [depth, 256])
    diff_norm_w = din("diff_norm_w", [depth, 128])
    w_proj_a = din("w_proj_a", [depth, D, D])
    w_proj_b = din("w_proj_b", [depth, D, D])
    w_out = din("w_out", [depth, D, D])
    final_norm_w = din("final_norm_w", [1, D])
    c_ident = din("c_ident", [128, 128])
    c_cm = din("c_cm", [128, 128])
    c_cl = din("c_cl", [128, 128])
    c_su = din("c_su", [128, 128])
    cos_p = din("cos_p", [TPROMPT, 32])
    sin_p = din("sin_p", [TPROMPT, 32])
    cos_s = din("cos_s", [TS, 32])
    sin_s = din("sin_s", [TS, 32])

    yp = dout("yp", [TPROMPT, D])
    ys = dout("ys", [NSMP, TS, D])
    kp = dout("kp", [depth, H, TPROMPT, 128])
    vp = dout("vp", [depth, H, TPROMPT, 128])
    sp_o = dout("ssmp", [depth, H, 128, 128])
    cp_o = dout("convp", [depth, 3, 3072])
    ks_o = dout("ks", [depth, NSMP, H, TS, 128])
    vs_o = dout("vs", [depth, NSMP, H, TS, 128])
    ss_o = dout("ssms", [depth, NSMP, H, 128, 128])
    cs_o = dout("convs", [depth, NSMP, 3, 3072])

    wbf = dscr("wbf", [depth, D, INC], BF16)
    wpa = dscr("wpa", [depth, D, D], BF16)
    wpb = dscr("wpb", [depth, D, D], BF16)
    wo = dscr("wo", [depth, D, D], BF16)
    mods = dscr("mods", [depth, 1 + NSMP, 3 * D], F32)

    seqs = []
    seqs.append(dict(name="p", T=TPROMPT, TT=256, TP=128, L=64, P=0, x=xp, cidx=0,
                     y=yp, ko=lambda l: kp[l], vo=lambda l: vp[l], so=lambda l: sp_o[l], co=lambda l: cp_o[l],
                     cos=cos_p, sin=sin_p, s0=None, cv0=None, ck=None, cvv=None,
                     xres=[dscr("xres_p0", [TPROMPT, D], F32), dscr("xres_p1", [TPROMPT, D], F32)],
                     kts=dscr("kts_p", [H, 128, TPROMPT], BF16), vas=dscr("vas_p", [TPROMPT, H, 130], BF16)))
    for b in range(NSMP):
        seqs.append(dict(name=f"s{b}", T=TS, TT=TS, TP=TS, L=TS, P=PAST, x=xs[b], cidx=1 + b,
                         y=ys[b], ko=(lambda l, b=b: ks_o[l, b]), vo=(lambda l, b=b: vs_o[l, b]),
                         so=(lambda l, b=b: ss_o[l, b]), co=(lambda l, b=b: cs_o[l, b]),
                         cos=cos_s, sin=sin_s, s0=(lambda l, b=b: state_ssm[l, b]), cv0=(lambda l, b=b: state_conv[l, b]),
                         ck=(lambda l, b=b: cache_k[l, b]), cvv=(lambda l, b=b: cache_v[l, b]),
                         xres=[dscr(f"xres_s{b}0", [TS, D], F32), dscr(f"xres_s{b}1", [TS, D], F32)],
                         kts=dscr(f"kts_s{b}", [H, 128, TS], BF16), vas=dscr(f"vas_s{b}", [TS, H, 130], BF16)))

    with ExitStack() as st:
        P = Prog(nc, st)

        def sb(name, shape, dt=F32):
            return st.enter_context(nc.sbuf_tensor(name, list(shape), dt))

        def psum(name, shape, dt=F32):
            return st.enter_context(nc.psum_tensor(name, list(shape), dt))

        def mm(out, lhsT, rhs, start=True, stop=True):
            P.op("pe", lambda: nc.tensor.matmul(out.ap, lhsT.ap, rhs.ap, start=start, stop=stop), [out], [lhsT, rhs], inc=stop)

        def tr(out, in_, ident):
            P.op("pe", lambda: nc.tensor.transpose(out.ap, in_.ap, ident.ap), [out], [in_, ident])

        def act(out, in_, func, scale=None, bias=None, extra=()):
            kw = {}
            if scale is not None:
                kw["scale"] = scale.ap if isinstance(scale, V) else scale
            if bias is not None:
                kw["bias"] = bias.ap if isinstance(bias, V) else bias
            ins = [in_] + [x for x in (scale, bias) if isinstance(x, V)] + list(extra)
            P.op("act", lambda: nc.scalar.activation(out.ap, in_.ap, func, **kw), [out], ins)

        def tt(eng, out, a, b, op):
            h = nc.vector if eng == "dve" else nc.gpsimd
            P.op(eng, lambda: h.tensor_tensor(out.ap, a.ap, b.ap, op), [out], [a, b])

        def ts(eng, out, a, s1, op0, s2=None, op1=None):
            h = nc.vector if eng == "dve" else nc.gpsimd
            ins = [a] + [x for x in (s1, s2) if isinstance(x, V)]
            s1a = s1.ap if isinstance(s1, V) else s1
            s2a = s2.ap if isinstance(s2, V) else s2
            if op1 is None:
                P.op(eng, lambda: h.tensor_scalar(out.ap, a.ap, s1a, None, op0), [out], ins)
            else:
                P.op(eng, lambda: h.tensor_scalar(out.ap, a.ap, s1a, s2a, op0, op1), [out], ins)

        def stt(out, a, s, b, op0, op1):
            ins = [a, b] + ([s] if isinstance(s, V) else [])
            sa = s.ap if isinstance(s, V) else s
            P.op("dve", lambda: nc.vector.scalar_tensor_tensor(out.ap, a.ap, sa, b.ap, op0, op1), [out], ins)

        def cp(eng, out, in_):
            if eng == "act":
                P.op("act", lambda: nc.scalar.copy(out.ap, in_.ap), [out], [in_])
            else:
                h = nc.vector if eng == "dve" else nc.gpsimd
                P.op(eng, lambda: h.tensor_copy(out.ap, in_.ap), [out], [in_])

        def memset(eng, out, val):
            h = nc.vector if eng == "dve" else nc.gpsimd
            P.op(eng, lambda: h.memset(out.ap, val), [out], [])

        def recip(out, in_):
            P.op("dve", lambda: nc.vector.reciprocal(out.ap, in_.ap), [out], [in_])

        def ttr(out, a, b, accum):
            P.op("dve", lambda: nc.vector.scalar_tensor_tensor(out.ap, a.ap, 1.0, b.ap, ALU.mult, ALU.mult, accum_out=accum.ap), [out, accum], [a, b])

        def redx(out, in_):
            P.op("dve", lambda: nc.vector.tensor_reduce(out.ap, in_.ap, AX.X, ALU.add), [out], [in_])

        identf = sb("identf", [128, 128]); identb = sb("identb", [128, 128], BF16)
        cm = sb("cm", [128, 128]); cl = sb("cl", [128, 128]); su = sb("su", [128, 128])
        onesb = sb("onesb", [128, 4], BF16)
        Videntf = V(identf[:], "identf"); Videntb = V(identb[:], "identb")
        P.dma("sp", Videntf, V(c_ident, "c_ident"))
        P.dma("sp", V(cm[:], "cm"), V(c_cm, "c_cm"))
        P.dma("sp", V(cl[:], "cl"), V(c_cl, "c_cl"))
        P.dma("sp", V(su[:], "su"), V(c_su, "c_su"))
        cp("dve", Videntb, Videntf)
        memset("dve", V(onesb[:], "onesb"), 1.0)

        psA = [psum(f"psA{i}", [128, 512]) for i in range(2)]
        psT = psum("psT", [128, 1024], BF16)
        psN = psum("psN", [128, 512])
        psG = [psum(f"psG{i}", [128, 512]) for i in range(2)]
        psC = [psum(f"psC{i}", [128, 512]) for i in range(2)]

        with ExitStack() as st0:
            stg = Ring(nc, st0, "stg", [128, KC, 512], F32, 2)
            wbr = Ring(nc, st0, "wbr", [128, KC, 512], BF16, 2)
            cT = st0.enter_context(nc.sbuf_tensor("cT", [128, KC, 4], F32))
            modr = st0.enter_context(nc.sbuf_tensor("modr", [4, 3 * D], F32))
            nseq = 1 + NSMP
            memset("dve", V(cT[:], "cT"), 0.0)
            for s in range(nseq):
                P.dma("sp", V(cT[:, :, s], "cT"), V(c_all[s].rearrange("(k p) -> p k", p=128), "c_all"),
                      allow_slow_non_contiguous=True)
            cnt = 0
            for l in range(depth):
                for (src, dst, ncol) in ((w_in[l], wbf[l], INC), (w_proj_a[l], wpa[l], D), (w_proj_b[l], wpb[l], D), (w_out[l], wo[l], D)):
                    srcv = src.rearrange("(k p) n -> p k n", p=128)
                    dstv = dst.rearrange("(k p) n -> p k n", p=128)
                    for c0 in range(0, ncol, 512):
                        w = min(512, ncol - c0)
                        t32, k32 = stg.next()
                        t16, k16 = wbr.next()
                        P.dma("sp", V(t32[:, :, :w], k32), V(srcv[:, :, c0:c0 + w], "wsrc"))
                        eng = ("dve", "act", "pool")[cnt % 3]
                        cnt += 1
                        cp(eng, V(t16[:, :, :w], k16), V(t32[:, :, :w], k32))
                        P.dma("pool", V(dstv[:, :, c0:c0 + w], "D:w16"), V(t16[:, :, :w], k16))
                wav = w_ada[l].rearrange("(k p) n -> p k n", p=128)
                for s in range(nseq):
                    P.dma("sp", V(modr[s:s + 1, :], "modr"), V(b_ada[l:l + 1, :], "b_ada"))
                for jc in range(6):
                    t32, k32 = stg.next()
                    P.dma("sp", V(t32[:], k32), V(wav[:, :, jc * 512:(jc + 1) * 512], "wsrc"))
                    for kc in range(KC):
                        mm(V(psA[0][0:nseq, :], "psA0"), V(cT[:, kc, 0:nseq], "cT"), V(t32[:, kc, :], k32), start=(kc == 0), stop=(kc == KC - 1))
                    tt("dve", V(modr[0:nseq, jc * 512:(jc + 1) * 512], "modr"), V(modr[0:nseq, jc * 512:(jc + 1) * 512], "modr"),
                       V(psA[0][0:nseq, :], "psA0"), ALU.add)
                P.dma("pool", V(mods[l], "D:mods"), V(modr[0:nseq, :], "modr"))
            P.barrier_sp()

        slabs = Ring(nc, st, "slab", [128, KC, 512], BF16, 2)
        X = sb("X", [128, 2, D])
        Hb = sb("Hb", [128, D], BF16)
        tmpA = sb("tmpA", [128, D]); tmpB = sb("tmpB", [128, D])
        hT = sb("hT", [128, KC, 256], BF16)
        Uc = Ring(nc, st, "Uc", [128, 3 + 256], F32, 2)
        accr = Ring(nc, st, "accr", [128, 256], F32, 2)
        sqr = Ring(nc, st, "sqr", [128, 256], BF16, 2)
        HALO = sb("HALO", [128, 24, 3])
        QKV = sb("QKV", [128, 24, 256], BF16)
        QB = sb("QB", [128, 2, D], BF16); KBb = sb("KBb", [128, 2, D], BF16)
        stgf = Ring(nc, st, "stgf", [128, D], F32, 2)
        VB = sb("VB", [128, 2, H, 130], BF16)
        QT = sb("QT", [128, H, 256], BF16)
        KTt = sb("KTt", [128, H, 256], BF16)
        KTr = Ring(nc, st, "KTr", [128, 1024], BF16, 2)
        VAr = Ring(nc, st, "VAr", [128, 8, 130], BF16, 2)
        PTr = Ring(nc, st, "PTr", [128, 2, 256], BF16, 3)
        S = sb("S", [128, H, 128]); Sbf = sb("Sbf", [128, H, 128], BF16)
        OAn = sb("OAn", [128, 2, D]); OBn = sb("OBn", [128, 2, D])
        OA = sb("OA", [128, 2, D], BF16); OB = sb("OB", [128, 2, D], BF16)
        SGA = sb("SGA", [128, 2, D], BF16); SGB = sb("SGB", [128, 2, D], BF16)
        OAT = QT; OBT = KTt
        MG = OAn; MGb = sb("MGb", [128, 2, D], BF16)
        modB = sb("modB", [128, 3 * D]); nwB = sb("nwB", [128, D])
        cw = sb("cw", [128, 24, 4])
        alB = sb("alB", [128, H]); dtB = sb("dtB", [128, H]); gnwB = sb("gnwB", [128, 128]); dnwB = sb("dnwB", [128, 128])
        lqB = sb("lqB", [128, 256]); lamc = sb("lamc", [128, 4]); fnwB = sb("fnwB", [128, D])
        cosT = sb("cosT", [128, 2, 32]); sinT = sb("sinT", [128, 2, 32])
        sm = sb("sm", [128, 256])
        gf = Ring(nc, st, "gf", [128, 128], F32, 2)
        gfn = {}

        def KO(nm, n, hs=None):
            hh = range(H) if hs is None else range(hs * 4, hs * 4 + 4)
            return [(nm, n, h_) for h_ in hh]
        TA = [("tmpA", 0), ("tmpA", 1)]; TB = [("tmpB", 0), ("tmpB", 1)]
        SK = [("S", h_) for h_ in range(H)]; SBK = [("Sbf", h_) for h_ in range(H)]
        HK = [("HALO", c_) for c_ in range(24)]

        def G32(tag, n=2):
            if tag not in gfn:
                gfn[tag] = Ring(nc, st, "g_" + tag, [128, 128], F32, n)
            return gfn[tag].next()

        gbn = {}

        def G16(tag, n=2):
            if tag not in gbn:
                gbn[tag] = Ring(nc, st, "b_" + tag, [128, 128], BF16, n)
            return gbn[tag].next()

        P.dma("sp", V(fnwB[:], "fnwB"), V(final_norm_w.partition_broadcast(128).rearrange("p o d -> p (o d)"), "fnw"))
        memset("pool", V(VB[:, :, :, 128:130], "VBones"), 1.0)

        def load_slab(wd, c0, w):
            t, k = slabs.next()
            P.dma("sp", V(t[:, :, :w], k), V(wd.rearrange("(k p) n -> p k n", p=128)[:, :, c0:c0 + w], "D:w16"))
            return t, k

        cpi = [0]

        def evac_eng():
            cpi[0] += 1
            return ("act", "dve")[cpi[0] % 2]

        try:
          stop_at(0)
          for l in range(depth):
              lam_init = lam_inits[l]
              for j in range(4):
                  P.dma("sp", V(cw[:, :, j], "cw"), V(conv_w[l, j].rearrange("(c p) -> p c", p=128), "conv_w"), allow_slow_non_contiguous=True)
              P.dma("sp", V(alB[:], "alB"), V(a_log[l:l + 1, :].partition_broadcast(128).rearrange("p o d -> p (o d)"), "a_log"))
              P.dma("sp", V(dtB[:], "dtB"), V(dt_bias[l:l + 1, :].partition_broadcast(128).rearrange("p o d -> p (o d)"), "dt_bias"))
              P.dma("sp", V(gnwB[:], "gnwB"), V(gdn_norm_w[l:l + 1, :].partition_broadcast(128).rearrange("p o d -> p (o d)"), "gnw"))
              P.dma("sp", V(dnwB[:], "dnwB"), V(diff_norm_w[l:l + 1, :].partition_broadcast(128).rearrange("p o d -> p (o d)"), "dnw"))
              P.dma("sp", V(lqB[:], "lqB"), V(lam_qk[l:l + 1, :].partition_broadcast(128).rearrange("p o d -> p (o d)"), "lam_qk"))
              P.dma("sp", V(nwB[:], "nwB"), V(norm_w[l:l + 1, :].partition_broadcast(128).rearrange("p o d -> p (o d)"), "norm_w"))
              act(V(alB[:], "alB"), V(alB[:], "alB"), AF.Exp)
              ts("dve", V(alB[:], "alB"), V(alB[:], "alB"), -1.0, ALU.mult)
              ts("dve", V(dnwB[:], "dnwB"), V(dnwB[:], "dnwB"), 1.0 - lam_init, ALU.mult)
              t0, k0 = G32("j")
              ttr(V(t0[:, 0:64], k0), V(lqB[:, 0:64], "lqB"), V(lqB[:, 64:128], "lqB"), V(lamc[:, 0:1], "lamc"))
              t0, k0 = G32("j")
              ttr(V(t0[:, 0:64], k0), V(lqB[:, 128:192], "lqB"), V(lqB[:, 192:256], "lqB"), V(lamc[:, 1:2], "lamc"))
              act(V(lamc[:, 0:2], "lamc"), V(lamc[:, 0:2], "lamc"), AF.Exp)
              tt("dve", V(lamc[:, 2:3], "lamc"), V(lamc[:, 1:2], "lamc"), V(lamc[:, 0:1], "lamc"), ALU.subtract)
              ts("dve", V(lamc[:, 2:3], "lamc"), V(lamc[:, 2:3], "lamc"), -lam_init, ALU.add)

              for sq in seqs:
                  T, TT, TP, L, PA = sq["T"], sq["TT"], sq["TP"], sq["L"], sq["P"]
                  NS = TT // TP
                  NCH = TP // L
                  nlev = 5 if L == 64 else 4
                  ntile = T // TT
                  last_layer = (l == depth - 1)
                  x_src = sq["x"] if l == 0 else sq["xres"][(l - 1) % 2]
                  x_dst = sq["xres"][l % 2]
                  if sq["s0"] is None:
                      memset("pool", V(S[:], SK), 0.0)
                      memset("pool", V(HALO[:], HK), 0.0)
                  else:
                      P.dma("sp", V(S[:], SK), V(sq["s0"](l).rearrange("h k v -> k h v"), "state_ssm"))
                      t0, k0 = modB, "modB"
                      P.dma("sp", V(t0[0:3, :], k0), V(sq["cv0"](l), "state_conv"))
                      for c in range(24):
                          tr(V(psN[:, c * 4:c * 4 + 3], "psN"), V(t0[0:3, c * 128:(c + 1) * 128], k0), V(identf[0:3, 0:3], "identf"))
                      cp("dve", V(HALO[:], HK), V(psN[:, 0:96].rearrange("p (c j) -> p c j", j=4)[:, :, 0:3], "psN"))
                  P.dma("sp", V(modB[:], "modB"), V(mods[l, sq["cidx"]:sq["cidx"] + 1, :].partition_broadcast(128).rearrange("p o d -> p (o d)"), "D:mods"))
                  stt(V(modB[:, D:2 * D], "modB"), V(modB[:, D:2 * D], "modB"), 1.0, V(nwB[:], "nwB"), ALU.add, ALU.mult)
                  shiftB = V(modB[:, 0:D], "modB"); wmodB = V(modB[:, D:2 * D], "modB")
                  cp("act", V(Sbf[:], SBK), V(S[:], SK))

                  for it in range(ntile):
                      tok0 = it * TT
                      P.dma("sp", V(X[:TP, :NS, :], "X"), V(x_src[tok0:tok0 + TT, :].rearrange("(n p) d -> p n d", p=TP), "D:xres" + sq["name"]))
                      P.dma("sp", V(cosT[:TP, :NS, :], "cosT"), V(sq["cos"][tok0:tok0 + TT, :].rearrange("(n p) d -> p n d", p=TP), "cos"))
                      P.dma("sp", V(sinT[:TP, :NS, :], "sinT"), V(sq["sin"][tok0:tok0 + TT, :].rearrange("(n p) d -> p n d", p=TP), "sin"))
                      for n in range(NS):
                          ttr(V(tmpA[:TP, :], TA), V(X[:TP, n, :], "X"), V(X[:TP, n, :], "X"), V(sm[:TP, n:n + 1], "sm_ss"))
                      ts("dve", V(sm[:TP, 0:NS], "sm_ss"), V(sm[:TP, 0:NS], "sm_ss"), 1.0 / D, ALU.mult, EPS, ALU.add)
                      act(V(sm[:TP, 0:NS], "sm_ss"), V(sm[:TP, 0:NS], "sm_ss"), AF.Sqrt)
                      recip(V(sm[:TP, 2:2 + NS], "sm_rs"), V(sm[:TP, 0:NS], "sm_ss"))
                      for n in range(NS):
                          stt(V(tmpA[:TP, :], TA), V(X[:TP, n, :], "X"), V(sm[:TP, 2 + n:3 + n], "sm_rs"), V(modB[:TP, D:2 * D], "modB"), ALU.mult, ALU.mult)
                          tt("pool", V(Hb[:TP, :], "Hb"), V(tmpA[:TP, :], TA), V(modB[:TP, 0:D], "modB"), ALU.add)
                          for g in range(2):
                              for j in range(4):
                                  kc = 4 * g + j
                                  tr(V(psT[:, g * 512 + j * 128:g * 512 + j * 128 + TP], ("psT", g)), V(Hb[:TP, kc * 128:(kc + 1) * 128], "Hb"), V(identb[:TP, :TP], "identb"))
                              cp(evac_eng(), V(hT[:, 4 * g:4 * g + 4, n * TP:(n + 1) * TP], "hT"),
                                 V(psT[:, g * 512:(g + 1) * 512].rearrange("p (j t) -> p j t", t=128)[:, :, :TP], ("psT", g)))

                      stop_at(1)
                      for s6 in range(6):
                          t, k = load_slab(wbf[l], s6 * 512, 512)
                          for cc in range(4):
                              c = 4 * s6 + cc
                              pa = psA[c % 2]; pk = f"psA{c % 2}"
                              for kc in range(KC):
                                  mm(V(pa[:, 0:TT], pk), V(t[:, kc, cc * 128:(cc + 1) * 128], k), V(hT[:, kc, 0:TT], "hT"), start=(kc == 0), stop=(kc == KC - 1))
                              u, uk = Uc.next()
                              cp("act", V(u[:, 3:3 + TT], uk), V(pa[:, 0:TT], pk))
                              cp("pool", V(u[:, 0:3], uk), V(HALO[:, c, :], ("HALO", c)))
                              a, ak = accr.next()
                              ts("dve", V(a[:, :TT], ak), V(u[:, 0:TT], uk), V(cw[:, c, 0:1], "cw"), ALU.mult)
                              for j in range(1, 4):
                                  stt(V(a[:, :TT], ak), V(u[:, j:j + TT], uk), V(cw[:, c, j:j + 1], "cw"), V(a[:, :TT], ak), ALU.mult, ALU.add)
                              cp("pool", V(HALO[:, c, :], ("HALO", c)), V(u[:, TT:TT + 3], uk))
                              act(V(QKV[:, c, :TT], ("QKV", c)), V(a[:, :TT], ak), AF.Silu)
                              if c < 16:
                                  q2, qk2 = sqr.next()
                                  act(V(q2[:, :TT], qk2), V(QKV[:, c, :TT], ("QKV", c)), AF.Square)
                                  for n in range(NS):
                                      mm(V(psN[:TP, 128 + n * 16 + c:128 + n * 16 + c + 1], "psN"), V(q2[:, n * TP:(n + 1) * TP], qk2), V(onesb[:, 0:1], "onesb"))
                      if it == ntile - 1:
                          nl = NS - 1
                          for s6 in range(6):
                              t, k = load_slab(wbf[l], s6 * 512, 512)
                              for kc in range(KC):
                                  mm(V(psA[0][:TP, :], "psA0"), V(hT[:, kc, nl * TP:(nl + 1) * TP], "hT"), V(t[:, kc, :], k), start=(kc == 0), stop=(kc == KC - 1))
                              pb = 64 if TP == 128 else 0
                              cp("act", V(tmpB[pb:TP, (s6 % 2) * 512:(s6 % 2) * 512 + 512], ("tmpB", s6 % 2)), V(psA[0][pb:TP, :], "psA0"))
                              P.dma("pool", V(sq["co"](l)[:, s6 * 512:(s6 + 1) * 512], "conv_out"), V(tmpB[TP - 3:TP, (s6 % 2) * 512:(s6 % 2) * 512 + 512], ("tmpB", s6 % 2)))

                      stop_at(2)
                      t, k = load_slab(wbf[l], C_BA, 16)
                      for n in range(NS):
                          for kc in range(KC):
                              mm(V(psN[:TP, 200 + n * 16:200 + n * 16 + 16], "psN"), V(hT[:, kc, n * TP:(n + 1) * TP], "hT"), V(t[:, kc, 0:16], k), start=(kc == 0), stop=(kc == KC - 1))
                      for n in range(NS):
                          bcol = V(sm[:TP, 8 + n * 8:16 + n * 8], "sm_beta"); gcol = V(sm[:TP, 24 + n * 8:32 + n * 8], "sm_g")
                          act(bcol, V(psN[:TP, 200 + n * 16:208 + n * 16], "psN"), AF.Sigmoid)
                          t1 = V(sm[:TP, 40:48], "sm_t1"); t2 = V(sm[:TP, 48:56], "sm_t2")
                          tt("dve", t1, V(psN[:TP, 208 + n * 16:216 + n * 16], "psN"), V(dtB[:TP, :], "dtB"), ALU.add)
                          stt(t2, t1, -1.0, t1, ALU.mult, ALU.min)
                          act(t2, t2, AF.Exp)
                          ts("dve", t2, t2, 1.0, ALU.add)
                          act(t2, t2, AF.Ln)
                          stt(t2, t1, 0.0, t2, ALU.max, ALU.add)
                          tt("dve", gcol, t2, V(alB[:TP, :], "alB"), ALU.mult)

                      stop_at(3)
                      for gi, (cbase, kind) in enumerate(((C_QB, "q"), (C_KB, "k"), (C_VB, "v"))):
                          for hs in range(2):
                              t, k = load_slab(wbf[l], cbase + hs * 512, 512)
                              for n in range(NS):
                                  pa = psA[(n + hs) % 2]; pk = f"psA{(n + hs) % 2}"
                                  for kc in range(KC):
                                      mm(V(pa[:TP, :], pk), V(hT[:, kc, n * TP:(n + 1) * TP], "hT"), V(t[:, kc, :], k), start=(kc == 0), stop=(kc == KC - 1))
                                  cols = slice(hs * 512, (hs + 1) * 512)
                                  if kind == "v":
                                      if hs == 0:
                                          sq["_vf%d" % n] = stgf.next()
                                      sf, sfk = sq["_vf%d" % n]
                                      cp("act", V(sf[:TP, cols], (sfk, hs)), V(pa[:TP, :], pk))
                                      cp("pool", V(VB[:TP, n, hs * 4:(hs + 1) * 4, 0:128], ("VB", n, hs)), V(sf[:TP, cols].rearrange("p (h d) -> p h d", d=128), (sfk, hs)))
                                      if hs == 1:
                                          P.dma("pool", V(sq["vo"](l)[:, tok0 + n * TP:tok0 + (n + 1) * TP, :].rearrange("h p d -> p h d"), "v_out"),
                                                V(sf[:TP, :].rearrange("p (h d) -> p h d", d=128), [(sfk, 0), (sfk, 1)]))
                                  else:
                                      pv = pa[:TP, :].rearrange("p (a h d) -> p a h d", h=2, d=32)
                                      x1 = V(pv[:, :, 0, :], pk); x2 = V(pv[:, :, 1, :], pk)
                                      cB = V(cosT[:TP, n:n + 1, :].broadcast_to([TP, 8, 32]), "cosT")
                                      sB = V(sinT[:TP, n:n + 1, :].broadcast_to([TP, 8, 32]), "sinT")
                                      ta = V(tmpA[:TP, 0:256].rearrange("p (a d) -> p a d", d=32), TA)
                                      tb = V(tmpA[:TP, 256:512].rearrange("p (a d) -> p a d", d=32), TA)
                                      tc = V(tmpA[:TP, 512:768].rearrange("p (a d) -> p a d", d=32), TA)
                                      td = V(tmpA[:TP, 768:1024].rearrange("p (a d) -> p a d", d=32), TA)
                                      tt("dve", ta, x1, cB, ALU.mult)
                                      tt("dve", tb, x2, sB, ALU.mult)
                                      tt("dve", tc, x1, sB, ALU.mult)
                                      tt("dve", td, x2, cB, ALU.mult)
                                      if kind == "q":
                                          ov = QB[:TP, n, cols].rearrange("p (a h d) -> p a h d", h=2, d=32)
                                          tt("pool", V(ov[:, :, 0, :], ("QB", n, hs)), ta, tb, ALU.subtract)
                                          tt("pool", V(ov[:, :, 1, :], ("QB", n, hs)), tc, td, ALU.add)
                                      else:
                                          if hs == 0:
                                              sq["_kf%d" % n] = stgf.next()
                                          sf, sfk = sq["_kf%d" % n]
                                          ov = sf[:TP, cols].rearrange("p (a h d) -> p a h d", h=2, d=32)
                                          tt("pool", V(ov[:, :, 0, :], (sfk, hs)), ta, tb, ALU.subtract)
                                          tt("pool", V(ov[:, :, 1, :], (sfk, hs)), tc, td, ALU.add)
                                          cp("act", V(KBb[:TP, n, cols], ("KBb", n, hs)), V(sf[:TP, cols], (sfk, hs)))
                                          if hs == 1:
                                              P.dma("pool", V(sq["ko"](l)[:, tok0 + n * TP:tok0 + (n + 1) * TP, :].rearrange("h p d -> p h d"), "k_out"),
                                                    V(sf[:TP, :].rearrange("p (h d) -> p h d", d=128), [(sfk, 0), (sfk, 1)]))
                      for n in range(NS):
                          for (src, sname, dst, dname) in ((QB, "QB", QT, "QT"), (KBb, "KBb", KTt, "KTt")):
                              for g in range(2):
                                  for j in range(4):
                                      hh = 4 * g + j
                                      tr(V(psT[:, g * 512 + j * 128:g * 512 + j * 128 + TP], ("psT", g)), V(src[:TP, n, hh * 128:(hh + 1) * 128], (sname, n, g)), V(identb[:TP, :TP], "identb"))
                                  cp(evac_eng(), V(dst[:, 4 * g:4 * g + 4, n * TP:(n + 1) * TP], dname),
                                     V(psT[:, g * 512:(g + 1) * 512].rearrange("p (j t) -> p j t", t=128)[:, :, :TP], ("psT", g)))
                          P.dma("pool", V(sq["vas"][tok0 + n * TP:tok0 + (n + 1) * TP, :, :], "D:vas" + sq["name"]), V(VB[:TP, n, :, :], [("VB", n, 0), ("VB", n, 1), "VBones"]))
                      P.dma("pool", V(sq["kts"][:, :, tok0:tok0 + TT].rearrange("h p t -> p h t"), "D:kts" + sq["name"]), V(KTt[:, :, :TT], "KTt"))

                      stop_at(4)
                      for n in range(NS):
                          bsl = slice(n * TP, (n + 1) * TP)
                          beta = sm[:TP, 8 + n * 8:16 + n * 8]; gcolap = sm[:TP, 24 + n * 8:32 + n * 8]
                          ik = sm[:TP, 64:72]; rk = sm[:TP, 72:80]; rq = sm[:TP, 80:88]; Gc = sm[:TP, 88:96]; GL = sm[:TP, 96:104]
                          nEG = sm[:TP, 104:112]; kds = sm[:TP, 112:120]; brk2 = sm[:TP, 120:128]; brk = sm[:TP, 128:136]; tq = sm[:TP, 136:144]
                          K = "sm_g2"
                          ts("dve", V(tq, K), V(psN[:TP, 128 + n * 16:136 + n * 16], "psN"), EPS, ALU.add)
                          act(V(tq, K), V(tq, K), AF.Sqrt)
                          recip(V(rq, K), V(tq, K))
                          ts("dve", V(rq, K), V(rq, K), 128 ** -0.5, ALU.mult)
                          ts("dve", V(ik, K), V(psN[:TP, 136 + n * 16:144 + n * 16], "psN"), EPS, ALU.add)
                          act(V(ik, K), V(ik, K), AF.Sqrt)
                          recip(V(rk, K), V(ik, K))
                          mm(V(psN[:TP, 300:308], "psN"), V(cm[:TP, :TP], "cm"), V(gcolap, "sm_g"))
                          mm(V(psN[:TP, 308:316], "psN"), V(cl[:TP, :TP], "cl"), V(gcolap, "sm_g"))
                          cp("dve", V(sm[:TP, 88:104], K), V(psN[:TP, 300:316], "psN"))
                          act(V(nEG, K), V(Gc, K), AF.Exp)
                          ts("dve", V(nEG, K), V(nEG, K), -1.0, ALU.mult)
                          tt("dve", V(kds, K), V(GL, K), V(Gc, K), ALU.subtract)
                          act(V(kds, K), V(kds, K), AF.Exp)
                          tt("dve", V(kds, K), V(kds, K), V(rk, K), ALU.mult)
                          tt("dve", V(brk, K), V(beta, "sm_beta"), V(rk, K), ALU.mult)
                          tt("dve", V(brk2, K), V(brk, K), V(rk, K), ALU.mult)
                          for h in range(H):
                              qTh = V(QKV[:, h, bsl], ("QKV", h)); kTh = V(QKV[:, 8 + h, bsl], ("QKV", 8 + h)); vTh = V(QKV[:, 16 + h, bsl], ("QKV", 16 + h))
                              pg = psG[h % 2]; pgk = f"psG{h % 2}"
                              pc = psC[h % 2]; pck = f"psC{h % 2}"
                              grep, grk = G32("grep")
                              cp("pool", V(grep[:TP, :], grk), V(sm[:TP, 24 + n * 8 + h:25 + n * 8 + h].broadcast_to([TP, 128]), "sm_g"))
                              mm(V(pg[:, 0:TP], (pgk, 0)), V(grep[:TP, :], grk), V(cm[:TP, :TP], "cm"))
                              EB, EBk = G32("EB")
                              act(V(EB[:, :TP], EBk), V(pg[:, 0:TP], (pgk, 0)), AF.Exp)
                              DT, DTk = G32("DT")
                              ts("dve", V(DT[:TP, :TP], DTk), V(pg[:TP, 0:TP], (pgk, 0)), V(sm[:TP, 88 + h:89 + h], K), ALU.subtract, 0.0, ALU.min)
                              act(V(DT[:TP, :TP], DTk), V(DT[:TP, :TP], DTk), AF.Exp)
                              mm(V(pg[:TP, 128:128 + TP], (pgk, 1)), kTh, kTh)
                              mm(V(pg[:TP, 256:256 + TP], (pgk, 2)), kTh, qTh)
                              Vm, Vmk = G32("Vm")
                              stt(V(Vm[:TP, :TP], Vmk), V(pg[:TP, 128:128 + TP], (pgk, 1)), V(sm[:TP, 120 + h:121 + h], K), V(DT[:TP, :TP], DTk), ALU.mult, ALU.mult)
                              tt("pool", V(Vm[:TP, :TP], Vmk), V(Vm[:TP, :TP], Vmk), V(su[:TP, :TP], "su"), ALU.mult)
                              QKm, QKmk = G16("QKm")
                              q32, q32k = G32("q32")
                              stt(V(q32[:TP, :TP], q32k), V(pg[:TP, 256:256 + TP], (pgk, 2)), V(sm[:TP, 72 + h:73 + h], K), V(DT[:TP, :TP], DTk), ALU.mult, ALU.mult)
                              tt("pool", V(QKm[:TP, :TP], QKmk), V(q32[:TP, :TP], q32k), V(cm[:TP, :TP], "cm"), ALU.mult)
                              R, Rk = G32("R", 3)
                              stt(V(R[:TP, :TP], Rk), V(Vm[:TP, :TP], Vmk), -1.0, V(identf[:TP, :TP], "identf"), ALU.mult, ALU.add)
                              tr(V(pg[:TP, 384:384 + TP], (pgk, 3)), V(Vm[:TP, :TP], Vmk), V(identf[:TP, :TP], "identf"))
                              PT_, PTk = G32("PT", 3)
                              cp("act", V(PT_[:TP, :TP], PTk), V(pg[:TP, 384:384 + TP], (pgk, 3)))
                              Pm, Pmk = Vm, Vmk
                              for lev in range(nlev):
                                  mm(V(pg[:TP, 128:128 + TP], (pgk, 1)), V(Pm[:TP, :TP], Pmk), V(PT_[:TP, :TP], PTk))
                                  if lev < nlev - 1:
                                      mm(V(pg[:TP, 256:256 + TP], (pgk, 2)), V(PT_[:TP, :TP], PTk), V(Pm[:TP, :TP], Pmk))
                                  PTn, PTnk = G32("PT", 3)
                                  cp("act", V(PTn[:TP, :TP], PTnk), V(pg[:TP, 128:128 + TP], (pgk, 1)))
                                  if lev < nlev - 1:
                                      Pn, Pnk = G32("Pm", 3)
                                      cp("act", V(Pn[:TP, :TP], Pnk), V(pg[:TP, 256:256 + TP], (pgk, 2)))
                                  mm(V(pg[:TP, 384:384 + TP], (pgk, 3)), V(PTn[:TP, :TP], PTnk), V(R[:TP, :TP], Rk))
                                  Rn, Rnk = G32("R", 3)
                                  tt("dve", V(Rn[:TP, :TP], Rnk), V(R[:TP, :TP], Rk), V(pg[:TP, 384:384 + TP], (pgk, 3)), ALU.add)
                                  R, Rk = Rn, Rnk
                                  PT_, PTk = PTn, PTnk
                                  if lev < nlev - 1:
                                      Pm, Pmk = Pn, Pnk
                              Rb, Rbk = G16("Rb")
                              cp("pool", V(Rb[:TP, :TP], Rbk), V(R[:TP, :TP], Rk))
                              qdT, qdk = G16("qdT")
                              tt("dve", V(qdT[:, :TP], qdk), qTh, V(EB[:, :TP], EBk), ALU.mult)
                              tr(V(psT[:TP, 0:128], ("psT", 0)), vTh, Videntb)
                              vs_, vsk = G16("vs")
                              ts("dve", V(vs_[:TP, :], vsk), V(psT[:TP, 0:128], ("psT", 0)), V(sm[:TP, 64 + h:65 + h], K), ALU.mult)
                              tr(V(psT[:TP, 512:640], ("psT", 1)), kTh, Videntb)
                              kd, kdk = G16("kd")
                              ts("dve", V(kd[:TP, :], kdk), V(psT[:TP, 512:640], ("psT", 1)), V(sm[:TP, 112 + h:113 + h], K), ALU.mult)
                              U2, U2k = G16("U2"); W_, Wk = G16("W")
                              for c in range(NCH):
                                  r = slice(c * L, (c + 1) * L)
                                  Sh = V(Sbf[:, h, :], ("Sbf", h))
                                  mm(V(pc[:TP, 0:128], (pck, 0)), kTh, Sh)
                                  stt(V(U2[r, :], U2k), V(pc[r, 0:128], (pck, 0)), V(sm[r, 104 + h:105 + h], K), V(vs_[r, :], vsk), ALU.mult, ALU.add)
                                  mm(V(pc[:TP, 128:256], (pck, 1)), V(Rb[r, :TP], Rbk), V(U2[r, :], U2k))
                                  ts("dve", V(W_[r, :], Wk), V(pc[r, 128:256], (pck, 1)), V(sm[r, 128 + h:129 + h], K), ALU.mult)
                                  mm(V(pc[:TP, 256:384], (pck, 2)), V(qdT[:, :TP], qdk), Sh, start=True, stop=False)
                                  mm(V(pc[:TP, 256:384], (pck, 2)), V(QKm[r, :TP], QKmk), V(W_[r, :], Wk), start=False, stop=True)
                                  act(V(OAn[r, n, h * 128:(h + 1) * 128], ("OAn", n, h)), V(pc[r, 256:384], (pck, 2)), AF.Identity, scale=V(sm[r, 80 + h:81 + h], K))
                                  mm(V(pc[:, 384:512], (pck, 3)), V(kd[r, :], kdk), V(W_[r, :], Wk))
                                  stt(V(S[:, h, :], ("S", h)), V(S[:, h, :], ("S", h)), V(EB[:, (c + 1) * L - 1:(c + 1) * L], EBk), V(pc[:, 384:512], (pck, 3)), ALU.mult, ALU.add)
                                  cp("act", V(Sbf[:, h, :], ("Sbf", h)), V(S[:, h, :], ("S", h)))
                          okeys = [("OAn", n, h) for h in range(H)]
                          tt("pool", V(tmpB[:TP, :], TB), V(OAn[:TP, n, :], okeys), V(OAn[:TP, n, :], okeys), ALU.mult)
                          redx(V(sm[:TP, 144:152], "sm_on"), V(tmpB[:TP, :].rearrange("p (h d) -> p h d", d=128), TB))
                          ts("dve", V(sm[:TP, 144:152], "sm_on"), V(sm[:TP, 144:152], "sm_on"), 1.0 / 128, ALU.mult, EPS, ALU.add)
                          act(V(sm[:TP, 144:152], "sm_on"), V(sm[:TP, 144:152], "sm_on"), AF.Sqrt)
                          recip(V(sm[:TP, 152:160], "sm_on2"), V(sm[:TP, 144:152], "sm_on"))
                          o3 = OAn[:TP, n, :].rearrange("p (h d) -> p h d", d=128)
                          tt("dve", V(o3, okeys), V(o3, okeys), V(sm[:TP, 152:160].unsqueeze(2).broadcast_to([TP, 8, 128]), "sm_on2"), ALU.mult)
                          tt("pool", V(o3, okeys), V(o3, okeys), V(gnwB[:TP, :].unsqueeze(1).broadcast_to([TP, 8, 128]), "gnwB"), ALU.mult)
                      if it == ntile - 1:
                          P.dma("pool", V(sq["so"](l).rearrange("h k v -> k h v"), "ssm_out"), V(S[:], SK))

                      stop_at(5)
                      nkt_new = (tok0 + TT) // TP if PA == 0 else 1
                      for h in range(H):
                          loaders = []
                          if PA > 0:
                              for p0 in range(0, PA, 1024):
                                  def ld_cache(p0=p0):
                                      npk = min(1024, PA - p0)
                                      nt8 = npk // 128
                                      kt_t, kt_k = KTr.next(); va_t, va_k = VAr.next()
                                      c1, c1k = tmpA[:].rearrange("p (t d) -> p t d", d=128), TA
                                      P.dma("sp", V(c1[:, :nt8, :], c1k), V(sq["ck"](l)[h, p0:p0 + npk, :].rearrange("(t p) d -> p t d", p=128), "cache_k"))
                                      kb16, kb16k = G16("kc16", 2)
                                      for t8 in range(nt8):
                                          cp("pool", V(kb16[:, :], kb16k), V(c1[:, t8, :], c1k))
                                          tr(V(psT[:, (t8 % 2) * 512:(t8 % 2) * 512 + 128], ("psT", t8 % 2)), V(kb16[:, :], kb16k), Videntb)
                                          cp(evac_eng(), V(kt_t[:, t8 * 128:(t8 + 1) * 128], kt_k), V(psT[:, (t8 % 2) * 512:(t8 % 2) * 512 + 128], ("psT", t8 % 2)))
                                          kb16, kb16k = G16("kc16", 2)
                                      c2, c2k = tmpB[:].rearrange("p (t d) -> p t d", d=128), TB
                                      P.dma("sp", V(c2[:, :nt8, :], c2k), V(sq["cvv"](l)[h, p0:p0 + npk, :].rearrange("(t p) d -> p t d", p=128), "cache_v"))
                                      cp("pool", V(va_t[:, :nt8, 0:128], va_k), V(c2[:, :nt8, :], c2k))
                                      memset("pool", V(va_t[:, :nt8, 128:130], va_k), 1.0)
                                      return (kt_t, kt_k, va_t, va_k, [(128, None)] * nt8)
                                  loaders.append(ld_cache)

                              def ld_new():
                                  kt_t, kt_k = KTr.next(); va_t, va_k = VAr.next()
                                  P.dma("sp", V(kt_t[:, :TS], kt_k), V(sq["kts"][h, :, :], "D:kts" + sq["name"]))
                                  P.dma("sp", V(va_t[:TS, 0, :], va_k), V(sq["vas"][:, h, :], "D:vas" + sq["name"]))
                                  return (kt_t, kt_k, va_t, va_k, [(TS, None)])
                              loaders.append(ld_new)
                          else:
                              nkeys = tok0 + TT
                              for p0 in range(0, nkeys, 1024):
                                  def ld_self(p0=p0):
                                      npk = min(1024, nkeys - p0)
                                      nt8 = npk // 128
                                      kt_t, kt_k = KTr.next(); va_t, va_k = VAr.next()
                                      P.dma("sp", V(kt_t[:, :npk], kt_k), V(sq["kts"][h, :, p0:p0 + npk], "D:kts" + sq["name"]))
                                      P.dma("sp", V(va_t[:, :nt8, :], va_k), V(sq["vas"][p0:p0 + npk, h, :].rearrange("(t p) d -> p t d", p=128), "D:vas" + sq["name"]))
                                      return (kt_t, kt_k, va_t, va_k, [(128, (p0 // 128) + j) for j in range(nt8)])
                                  loaders.append(ld_self)
                          npieces = len(loaders)
                          stop_at(50)
                          ngrp = NS
                          acc_k = lambda g, m: ("psCacc", g, m)
                          first = [True] * ngrp
                          nxt_piece = loaders[0]()
                          for pi_ in range(npieces):
                              (kt_t, kt_k, va_t, va_k, tiles) = nxt_piece
                              if pi_ + 1 < npieces:
                                  nxt_piece = loaders[pi_ + 1]()
                              for j, (nk, gkt) in enumerate(tiles):
                                  if gkt is None:
                                      g0 = 0
                                  else:
                                      g0 = max(0, gkt - (tok0 // TP))
                                      if g0 >= ngrp:
                                          continue
                                  q0 = g0 * TP
                                  ncol = TT - q0
                                  sbk = [(psG[0], "psG0"), (psG[1], "psG1")] if j % 2 == 0 else [(psA[0], "psA0"), (psA[1], "psA1")]
                                  stop_at(51)
                                  pt, ptk = PTr.next()
                                  for m in range(2):
                                      stb, stk = sbk[m]
                                      mm(V(stb[:nk, q0:TT], stk), V(kt_t[m * 64:(m + 1) * 64, j * 128:j * 128 + nk], kt_k),
                                         V(QT[m * 64:(m + 1) * 64, h, q0:TT], "QT"))
                                  for m in range(2):
                                      stb, stk = sbk[m]
                                      act(V(pt[:nk, m, q0:TT], ptk), V(stb[:nk, q0:TT], stk), AF.Exp, scale=0.125)
                                  if gkt is not None and gkt >= tok0 // TP:
                                      memset("pool", V(pt[64:128, :, q0:q0 + 64], ptk), 0.0)
                                  stop_at(52)
                                  for g in range(g0, ngrp):
                                      lastk = (gkt is None and (pi_ == npieces - 1) and (j == len(tiles) - 1)) or (gkt is not None and gkt == tok0 // TP + g)
                                      for m in range(2):
                                          mm(V(psC[g][:TP, m * 256:m * 256 + 129], [(f"psC{g}", 2 * m), (f"psC{g}", 2 * m + 1)]), V(pt[:nk, m, g * TP:(g + 1) * TP], ptk), V(va_t[:nk, j, 0:129], va_k),
                                             start=(first[g] and m == 0), stop=lastk)
                                      first[g] = False
                          stop_at(53)
                          for g in range(ngrp):
                              dn = V(sm[:TP, 160:162], "sm_dn")
                              cp("dve", V(sm[:TP, 160:161], "sm_dn"), V(psC[g][:TP, 128:129], [(f"psC{g}", 0), (f"psC{g}", 1)]))
                              cp("dve", V(sm[:TP, 161:162], "sm_dn"), V(psC[g][:TP, 256 + 128:256 + 129], [(f"psC{g}", 2), (f"psC{g}", 3)]))
                              recip(V(sm[:TP, 162:164], ["sm_dr", "sm_dr2"]), dn)
                              tt("dve", V(sm[:TP, 163:164], "sm_dr2"), V(sm[:TP, 163:164], "sm_dr2"), V(lamc[:TP, 2:3], "lamc"), ALU.mult)
                              t1, t1k = G32("att1")
                              act(V(t1[:TP, :], t1k), V(psC[g][:TP, 0:128], [(f"psC{g}", 0), (f"psC{g}", 1)]), AF.Identity, scale=V(sm[:TP, 162:163], "sm_dr"))
                              stt(V(OBn[:TP, g, h * 128:(h + 1) * 128], ("OBn", g, h)), V(psC[g][:TP, 256:384], [(f"psC{g}", 2), (f"psC{g}", 3)]), V(sm[:TP, 163:164], "sm_dr2"),
                                  V(t1[:TP, :], t1k), ALU.mult, ALU.add)
                      stop_at(54)
                      for n in range(NS):
                          okeys = [("OBn", n, h) for h in range(H)]
                          tt("pool", V(tmpB[:TP, :], TB), V(OBn[:TP, n, :], okeys), V(OBn[:TP, n, :], okeys), ALU.mult)
                          redx(V(sm[:TP, 144:152], "sm_on"), V(tmpB[:TP, :].rearrange("p (h d) -> p h d", d=128), TB))
                          ts("dve", V(sm[:TP, 144:152], "sm_on"), V(sm[:TP, 144:152], "sm_on"), 1.0 / 128, ALU.mult, EPS, ALU.add)
                          act(V(sm[:TP, 144:152], "sm_on"), V(sm[:TP, 144:152], "sm_on"), AF.Sqrt)
                          recip(V(sm[:TP, 152:160], "sm_on2"), V(sm[:TP, 144:152], "sm_on"))
                          o3 = OBn[:TP, n, :].rearrange("p (h d) -> p h d", d=128)
                          tt("dve", V(o3, okeys), V(o3, okeys), V(sm[:TP, 152:160].unsqueeze(2).broadcast_to([TP, 8, 128]), "sm_on2"), ALU.mult)
                          tt("pool", V(o3, okeys), V(o3, okeys), V(dnwB[:TP, :].unsqueeze(1).broadcast_to([TP, 8, 128]), "dnwB"), ALU.mult)

                      stop_at(6)
                      for (cbase, kind) in ((C_ZA, "za"), (C_ZB, "zb"), (C_GA, "ga"), (C_GB, "gb")):
                          for hs in range(2):
                              t, k = load_slab(wbf[l], cbase + hs * 512, 512)
                              cols = slice(hs * 512, (hs + 1) * 512)
                              for n in range(NS):
                                  pa = psA[(n + hs) % 2]; pk = f"psA{(n + hs) % 2}"
                                  for kc in range(KC):
                                      mm(V(pa[:TP, :], pk), V(hT[:, kc, n * TP:(n + 1) * TP], "hT"), V(t[:, kc, :], k), start=(kc == 0), stop=(kc == KC - 1))
                                  if kind in ("za", "zb"):
                                      src = OAn if kind == "za" else OBn
                                      dst = OA if kind == "za" else OB
                                      nm = "OAn" if kind == "za" else "OBn"
                                      zt = V(tmpA[:TP, cols], ("tmpA", hs))
                                      act(zt, V(pa[:TP, :], pk), AF.Silu)
                                      tt("dve", V(dst[:TP, n, cols], (kind, n, hs)), zt, V(src[:TP, n, cols], [(nm, n, hh) for hh in range(hs * 4, hs * 4 + 4)]), ALU.mult)
                                  else:
                                      dst = SGA if kind == "ga" else SGB
                                      act(V(dst[:TP, n, cols], (kind, n, hs)), V(pa[:TP, :], pk), AF.Sigmoid)

                      stop_at(7)
                      for n in range(NS):
                          for (src, sname, dst, dname) in ((OA, "za", OAT, "QT"), (OB, "zb", OBT, "KTt")):
                              for g in range(2):
                                  for j in range(4):
                                      kc = 4 * g + j
                                      tr(V(psT[:, g * 512 + j * 128:g * 512 + j * 128 + TP], ("psT", g)), V(src[:TP, n, kc * 128:(kc + 1) * 128], (sname, n, g)), V(identb[:TP, :TP], "identb"))
                                  cp(evac_eng(), V(dst[:, 4 * g:4 * g + 4, n * TP:(n + 1) * TP], dname),
                                     V(psT[:, g * 512:(g + 1) * 512].rearrange("p (j t) -> p j t", t=128)[:, :, :TP], ("psT", g)))
                      for hs in range(2):
                          cols = slice(hs * 512, (hs + 1) * 512)
                          ta_, ka_ = load_slab(wpa[l], hs * 512, 512)
                          tb_, kb_ = load_slab(wpb[l], hs * 512, 512)
                          for n in range(NS):
                              for kc in range(KC):
                                  mm(V(psA[0][:TP, :], "psA0"), V(OAT[:, kc, n * TP:(n + 1) * TP], "QT"), V(ta_[:, kc, :], ka_), start=(kc == 0), stop=(kc == KC - 1))
                              for kc in range(KC):
                                  mm(V(psA[1][:TP, :], "psA1"), V(OBT[:, kc, n * TP:(n + 1) * TP], "KTt"), V(tb_[:, kc, :], kb_), start=(kc == 0), stop=(kc == KC - 1))
                              tt("dve", V(MG[:TP, n, cols], KO("OAn", n, hs)), V(psA[0][:TP, :], "psA0"), V(SGA[:TP, n, cols], ("ga", n, hs)), ALU.mult)
                              tt("dve", V(tmpA[:TP, cols], ("tmpA", hs)), V(psA[1][:TP, :], "psA1"), V(SGB[:TP, n, cols], ("gb", n, hs)), ALU.mult)
                              tt("pool", V(MGb[:TP, n, cols], ("MGb", n, hs)), V(MG[:TP, n, cols], KO("OAn", n, hs)), V(tmpA[:TP, cols], ("tmpA", hs)), ALU.add)
                      MT = hT
                      for n in range(NS):
                          for g in range(2):
                              for j in range(4):
                                  kc = 4 * g + j
                                  tr(V(psT[:, g * 512 + j * 128:g * 512 + j * 128 + TP], ("psT", g)), V(MGb[:TP, n, kc * 128:(kc + 1) * 128], ("MGb", n, g)), V(identb[:TP, :TP], "identb"))
                              cp(evac_eng(), V(MT[:, 4 * g:4 * g + 4, n * TP:(n + 1) * TP], "hT"),
                                 V(psT[:, g * 512:(g + 1) * 512].rearrange("p (j t) -> p j t", t=128)[:, :, :TP], ("psT", g)))
                      for hs in range(2):
                          cols = slice(hs * 512, (hs + 1) * 512)
                          to_, ko_ = load_slab(wo[l], hs * 512, 512)
                          for n in range(NS):
                              pa = psA[(n + hs) % 2]; pk = f"psA{(n + hs) % 2}"
                              for kc in range(KC):
                                  mm(V(pa[:TP, :], pk), V(MT[:, kc, n * TP:(n + 1) * TP], "hT"), V(to_[:, kc, :], ko_), start=(kc == 0), stop=(kc == KC - 1))
                              tt("dve", V(MG[:TP, n, cols], KO("OAn", n, hs)), V(pa[:TP, :], pk), V(modB[:TP, 2 * D + hs * 512:2 * D + (hs + 1) * 512], "modB"), ALU.mult)
                              tt("pool", V(MG[:TP, n, cols], KO("OAn", n, hs)), V(MG[:TP, n, cols], KO("OAn", n, hs)), V(X[:TP, n, cols], "X"), ALU.add)
                      mgk = [k_ for n_ in range(NS) for k_ in KO("OAn", n_)]
                      if not last_layer:
                          P.dma("pool", V(x_dst[tok0:tok0 + TT, :].rearrange("(n p) d -> p n d", p=TP), "D:xres" + sq["name"]), V(MG[:TP, :NS, :], mgk))
                      else:
                          for n in range(NS):
                              ttr(V(tmpA[:TP, :], TA), V(MG[:TP, n, :], mgk), V(MG[:TP, n, :], mgk), V(sm[:TP, n:n + 1], "sm_ss"))
                          ts("dve", V(sm[:TP, 0:NS], "sm_ss"), V(sm[:TP, 0:NS], "sm_ss"), 1.0 / D, ALU.mult, EPS, ALU.add)
                          act(V(sm[:TP, 0:NS], "sm_ss"), V(sm[:TP, 0:NS], "sm_ss"), AF.Sqrt)
                          recip(V(sm[:TP, 2:2 + NS], "sm_rs"), V(sm[:TP, 0:NS], "sm_ss"))
                          for n in range(NS):
                              stt(V(MG[:TP, n, :], mgk), V(MG[:TP, n, :], mgk), V(sm[:TP, 2 + n:3 + n], "sm_rs"), V(fnwB[:TP, :], "fnwB"), ALU.mult, ALU.mult)
                          P.dma("pool", V(sq["y"][tok0:tok0 + TT, :].rearrange("(n p) d -> p n d", p=TP), "y_out"), V(MG[:TP, :NS, :], mgk))

        except _Stop:
            pass
        P.finish()
        print("bass ops:", P.nops)
    return nc


_CACHE = {}


def _consts(T, P):
    ident = np.eye(128, dtype=np.float32)
    idx = np.arange(128)
    same = (idx[:, None] // 64) == (idx[None, :] // 64)
    cmm = (same & (idx[:, None] <= idx[None, :])).astype(np.float32)
    clm = same.astype(np.float32)
    sum_ = (same & (idx[:, None] < idx[None, :])).astype(np.float32)
    half = 32
    inv = (10000.0 ** (-np.arange(half, dtype=np.float32) / half)).astype(np.float32)

    def cs(pos):
        ang = pos.astype(np.float32)[:, None] * inv[None, :]
        return np.cos(ang).astype(np.float32), np.sin(ang).astype(np.float32)

    cp_, sp_ = cs(np.arange(T))
    cs_, ss_ = cs(P + np.arange(32))
    return dict(c_ident=ident, c_cm=cmm, c_cl=clm, c_su=sum_, cos_p=cp_, sin_p=sp_, cos_s=cs_, sin_s=ss_)


def run(inputs, n_cores=8):
    x_prompt = np.asarray(inputs["x_prompt"]); x_sample = np.asarray(inputs["x_sample"])
    B, T, _ = x_prompt.shape
    DB = x_sample.shape[0]
    depth = inputs["w_in"].shape[0]
    PAST = inputs["cache_k"].shape[3]
    ns = DB // n_cores
    cfg = dict(depth=depth, T=T, P=PAST, ns=ns)
    key = (depth, T, PAST, ns)
    if key not in _CACHE:
        _CACHE[key] = build(cfg)
    nc = _CACHE[key]
    consts = _consts(T, PAST)
    f = lambda a: np.ascontiguousarray(np.asarray(a, dtype=np.float32))
    shared = {k: f(inputs[k]) for k in ("norm_w", "w_ada", "b_ada", "w_in", "conv_w", "a_log", "dt_bias", "gdn_norm_w", "diff_norm_w", "w_proj_a", "w_proj_b", "w_out")}
    shared["lam_qk"] = f(inputs["lam_qk"]).reshape(depth, 256)
    shared["final_norm_w"] = f(inputs["final_norm_w"]).reshape(1, D)
    shared.update(consts)
    in_maps = []
    for c in range(n_cores):
        b = c % B
        sl = slice(c * ns, (c + 1) * ns)
        m = dict(shared)
        m["xp"] = f(x_prompt[b])
        m["xs"] = f(x_sample[sl])
        m["c_all"] = f(np.concatenate([np.asarray(inputs["c_prompt"])[b:b + 1], np.asarray(inputs["c_sample"])[sl]], axis=0))
        m["cache_k"] = f(np.asarray(inputs["cache_k"])[:, sl])
        m["cache_v"] = f(np.asarray(inputs["cache_v"])[:, sl])
        m["state_ssm"] = f(np.asarray(inputs["state_ssm"])[:, sl])
        m["state_conv"] = f(np.asarray(inputs["state_conv"])[:, sl])
        in_maps.append(m)
    res = run_bass_kernel_spmd(nc, in_maps, core_ids=list(range(n_cores)))
    R = res.results
    y_prompt = np.stack([R[b]["yp"] for b in range(B)], 0)
    y_sample = np.concatenate([R[c]["ys"] for c in range(n_cores)], 0)
    k_prompt = np.stack([R[b]["kp"] for b in range(B)], 1)
    v_prompt = np.stack([R[b]["vp"] for b in range(B)], 1)
    ssm_prompt = np.stack([R[b]["ssmp"] for b in range(B)], 1)
    conv_prompt = np.stack([R[b]["convp"] for b in range(B)], 1)
    k_sample = np.concatenate([R[c]["ks"] for c in range(n_cores)], 1)
    v_sample = np.concatenate([R[c]["vs"] for c in range(n_cores)], 1)
    ssm_sample = np.concatenate([R[c]["ssms"] for c in range(n_cores)], 1)
    conv_sample = np.concatenate([R[c]["convs"] for c in range(n_cores)], 1)
    outs = (y_prompt, y_sample, k_prompt, v_prompt, ssm_prompt, conv_prompt, k_sample, v_sample, ssm_sample, conv_sample)
    return tuple(np.ascontiguousarray(o, dtype=np.float32) for o in outs)


def kernel(**inputs):
    return run(inputs, n_cores=8)
```

```python
import math
from contextlib import ExitStack
import numpy as np
import ml_dtypes
import concourse.bass as bass
import concourse.mybir as mybir
from concourse.bass_utils import run_bass_kernel_spmd

F32 = mybir.dt.float32
BF16 = mybir.dt.bfloat16
AF = mybir.ActivationFunctionType
ALU = mybir.AluOpType
AX = mybir.AxisListType

D = 1024
KC = 8
H = 8
INC = 10256
EPS = 1e-6
C_ZA, C_BA, C_QB, C_KB, C_VB, C_ZB, C_GA, C_GB = 3072, 4096, 4112, 5136, 6160, 7184, 8208, 9232


def _normkey(k):
    if isinstance(k, tuple) and isinstance(k[0], str) and k[0].startswith("ps"):
        return k[0]
    return k


class V:
    __slots__ = ("ap", "keys")

    def __init__(self, ap, keys):
        self.ap = ap
        if isinstance(keys, list):
            ks = keys
        else:
            ks = [keys]
        out = []
        for k in ks:
            k = _normkey(k)
            if k not in out:
                out.append(k)
        self.keys = out


class Prog:
    def __init__(self, nc, stack, ndsem=48):
        self.nc = nc
        self.E = {}
        for name, h in [("pe", nc.tensor), ("dve", nc.vector), ("act", nc.scalar), ("pool", nc.gpsimd), ("sp", nc.sync)]:
            sem = stack.enter_context(nc.semaphore("s_" + name))
            self.E[name] = dict(h=h, sem=sem, cnt=0, known={}, name="s_" + name)
        self.ds = []
        for i in range(ndsem):
            sem = stack.enter_context(nc.semaphore(f"sd{i}"))
            self.ds.append(dict(sem=sem, cnt=0, name=f"sd{i}"))
        self.dnext = 0
        self.lastw = {}
        self.reads = {}
        self.nops = 0

    def _wait(self, eng, ev, raw=True):
        name, sem, val, src, snap = ev
        if src == eng and (eng == "pe" or not raw):
            return
        e = self.E[eng]
        kn = e["known"]
        if kn.get(name, 0) >= val:
            return
        e["h"].wait_ge(sem, val)
        kn[name] = val
        if snap:
            for k2, v2 in snap.items():
                if kn.get(k2, 0) < v2:
                    kn[k2] = v2

    def _deps(self, outs, ins, nraw=0):
        deps = []
        for v in ins:
            for k in v.keys:
                ev = self.lastw.get(k)
                if isinstance(ev, dict):
                    deps.extend((e_, True) for e_ in ev.values())
                elif ev is not None:
                    deps.append((ev, True))
        for i, v in enumerate(outs):
            israw = i < nraw
            for k in v.keys:
                ev = self.lastw.get(k)
                if isinstance(ev, dict):
                    deps.extend((e_, israw) for e_ in ev.values())
                elif ev is not None:
                    deps.append((ev, israw))
                r = self.reads.get(k)
                if r:
                    deps.extend((e_, False) for e_ in r.values())
        return deps

    def _reg(self, ev, outs, ins):
        for v in ins:
            for k in v.keys:
                r = self.reads.setdefault(k, {})
                if ev[3] == "dma":
                    r[ev[0]] = ev
                else:
                    r[ev[3]] = ev
        for v in outs:
            for k in v.keys:
                if isinstance(k, str) and k.startswith("D:"):
                    self.lastw.setdefault(k, {})[ev[0]] = ev
                else:
                    self.lastw[k] = ev
                    self.reads[k] = {}

    def barrier_sp(self):
        for slot in self.ds:
            if slot["cnt"] > 0:
                self._wait("sp", (slot["name"], slot["sem"], slot["cnt"], "dma", None))

    def op(self, eng, fn, outs, ins, inc=True):
        xin = [v for v in ins if any(isinstance(k, str) and k.startswith("ps") for k in v.keys)]
        if xin:
            ins = [v for v in ins if v not in xin]
            outs = xin + list(outs)
        for ev, israw in self._deps(outs, ins, len(xin)):
            self._wait(eng, ev, israw)
        inst = fn()
        e = self.E[eng]
        if inc:
            e["cnt"] += 1
            inst.then_inc(e["sem"], 1)
            ev = (e["name"], e["sem"], e["cnt"], eng, dict(e["known"]))
        else:
            ev = (e["name"], e["sem"], e["cnt"] + 1, eng, None)
        self._reg(ev, outs, ins)
        self.nops += 1

    def dma(self, q, out, in_, **kw):
        deps = self._deps([out], [in_])
        slot = self.ds[self.dnext]
        self.dnext = (self.dnext + 1) % len(self.ds)
        if slot["cnt"] > 0:
            self._wait(q, (slot["name"], slot["sem"], slot["cnt"], "dma", None))
        for ev, israw in deps:
            self._wait(q, ev, True)
        inst = self.E[q]["h"].dma_start(out=out.ap, in_=in_.ap, **kw)
        slot["cnt"] += 16
        inst.then_inc(slot["sem"], 16)
        ev = (slot["name"], slot["sem"], slot["cnt"], "dma", dict(self.E[q]["known"]))
        self._reg(ev, [out], [in_])
        self.nops += 1

    def finish(self):
        for slot in self.ds:
            if slot["cnt"] > 0:
                self._wait("sp", (slot["name"], slot["sem"], slot["cnt"], "dma", None))
        for en in ("pe", "dve", "act", "pool"):
            e = self.E[en]
            if e["cnt"] > 0:
                self._wait("sp", (e["name"], e["sem"], e["cnt"], en, None))


class Ring:
    def __init__(self, nc, stack, name, shape, dtype, n):
        self.ts = [stack.enter_context(nc.sbuf_tensor(f"{name}{i}", shape, dtype)) for i in range(n)]
        self.name = name
        self.i = 0

    def next(self):
        j = self.i % len(self.ts)
        self.i += 1
        return self.ts[j], (self.name, j)


class _Stop(Exception):
    pass


def build(cfg):
    import os
    KSTOP = int(os.environ.get('KSTOP', '99'))

    def stop_at(k):
        if KSTOP == k:
            raise _Stop()
    depth = cfg["depth"]
    TPROMPT = cfg["T"]
    PAST = cfg["P"]
    NSMP = cfg["ns"]
    TS = 32
    lam_inits = [0.8 - 0.6 * math.exp(-0.3 * l) for l in range(depth)]

    nc = bass.Bass("TRN2", target_bir_lowering=False)

    def din(name, shape, dt=F32):
        return nc.dram_tensor(name, list(shape), dt, kind="ExternalInput").ap()

    def dout(name, shape, dt=F32):
        return nc.dram_tensor(name, list(shape), dt, kind="ExternalOutput").ap()

    def dscr(name, shape, dt):
        return nc.dram_tensor(name, list(shape), dt, kind="Internal").ap()

    xp = din("xp", [TPROMPT, D])
    xs = din("xs", [NSMP, TS, D])
    c_all = din("c_all", [1 + NSMP, D])
    cache_k = din("cache_k", [depth, NSMP, H, PAST, 128])
    cache_v = din("cache_v", [depth, NSMP, H, PAST, 128])
    state_ssm = din("state_ssm", [depth, NSMP, H, 128, 128])
    state_conv = din("state_conv", [depth, NSMP, 3, 3072])
    norm_w = din("norm_w", [depth, D])
    w_ada = din("w_ada", [depth, D, 3 * D])
    b_ada = din("b_ada", [depth, 3 * D])
    w_in = din("w_in", [depth, D, INC])
    conv_w = din("conv_w", [depth, 4, 3072])
    a_log = din("a_log", [depth, H])
    dt_bias = din("dt_bias", [depth, H])
    gdn_norm_w = din("gdn_norm_w", [depth, 128])
    lam_qk = din("lam_qk", [depth, 256])
    diff_norm_w = din("diff_norm_w", [depth, 128])
    w_proj_a = din("w_proj_a", [depth, D, D])
    w_proj_b = din("w_proj_b", [depth, D, D])
    w_out = din("w_out", [depth, D, D])
    final_norm_w = din("final_norm_w", [1, D])
    c_ident = din("c_ident", [128, 128])
    c_cm = din("c_cm", [128, 128])
    c_cl = din("c_cl", [128, 128])
    c_su = din("c_su", [128, 128])
    cos_p = din("cos_p", [TPROMPT, 32])
    sin_p = din("sin_p", [TPROMPT, 32])
    cos_s = din("cos_s", [TS, 32])
    sin_s = din("sin_s", [TS, 32])

    yp = dout("yp", [TPROMPT, D])
    ys = dout("ys", [NSMP, TS, D])
    kp = dout("kp", [depth, H, TPROMPT, 128])
    vp = dout("vp", [depth, H, TPROMPT, 128])
    sp_o = dout("ssmp", [depth, H, 128, 128])
    cp_o = dout("convp", [depth, 3, 3072])
    ks_o = dout("ks", [depth, NSMP, H, TS, 128])
    vs_o = dout("vs", [depth, NSMP, H, TS, 128])
    ss_o = dout("ssms", [depth, NSMP, H, 128, 128])
    cs_o = dout("convs", [depth, NSMP, 3, 3072])

    wbf = dscr("wbf", [depth, D, INC], BF16)
    wpa = dscr("wpa", [depth, D, D], BF16)
    wpb = dscr("wpb", [depth, D, D], BF16)
    wo = dscr("wo", [depth, D, D], BF16)
    mods = dscr("mods", [depth, 1 + NSMP, 3 * D], F32)

    seqs = []
    seqs.append(dict(name="p", T=TPROMPT, TT=256, TP=128, L=64, P=0, x=xp, cidx=0,
                     y=yp, ko=lambda l: kp[l], vo=lambda l: vp[l], so=lambda l: sp_o[l], co=lambda l: cp_o[l],
                     cos=cos_p, sin=sin_p, s0=None, cv0=None, ck=None, cvv=None,
                     xres=[dscr("xres_p0", [TPROMPT, D], F32), dscr("xres_p1", [TPROMPT, D], F32)],
                     kts=dscr("kts_p", [H, 128, TPROMPT], BF16), vas=dscr("vas_p", [TPROMPT, H, 130], BF16)))
    for b in range(NSMP):
        seqs.append(dict(name=f"s{b}", T=TS, TT=TS, TP=TS, L=TS, P=PAST, x=xs[b], cidx=1 + b,
                         y=ys[b], ko=(lambda l, b=b: ks_o[l, b]), vo=(lambda l, b=b: vs_o[l, b]),
                         so=(lambda l, b=b: ss_o[l, b]), co=(lambda l, b=b: cs_o[l, b]),
                         cos=cos_s, sin=sin_s, s0=(lambda l, b=b: state_ssm[l, b]), cv0=(lambda l, b=b: state_conv[l, b]),
                         ck=(lambda l, b=b: cache_k[l, b]), cvv=(lambda l, b=b: cache_v[l, b]),
                         xres=[dscr(f"xres_s{b}0", [TS, D], F32), dscr(f"xres_s{b}1", [TS, D], F32)],
                         kts=dscr(f"kts_s{b}", [H, 128, TS], BF16), vas=dscr(f"vas_s{b}", [TS, H, 130], BF16)))

    with ExitStack() as st:
        P = Prog(nc, st)

        def sb(name, shape, dt=F32):
            return st.enter_context(nc.sbuf_tensor(name, list(shape), dt))

        def psum(name, shape, dt=F32):
            return st.enter_context(nc.psum_tensor(name, list(shape), dt))

        def mm(out, lhsT, rhs, start=True, stop=True):
            P.op("pe", lambda: nc.tensor.matmul(out.ap, lhsT.ap, rhs.ap, start=start, stop=stop), [out], [lhsT, rhs], inc=stop)

        def tr(out, in_, ident):
            P.op("pe", lambda: nc.tensor.transpose(out.ap, in_.ap, ident.ap), [out], [in_, ident])

        def act(out, in_, func, scale=None, bias=None, extra=()):
            kw = {}
            if scale is not None:
                kw["scale"] = scale.ap if isinstance(scale, V) else scale
            if bias is not None:
                kw["bias"] = bias.ap if isinstance(bias, V) else bias
            ins = [in_] + [x for x in (scale, bias) if isinstance(x, V)] + list(extra)
            P.op("act", lambda: nc.scalar.activation(out.ap, in_.ap, func, **kw), [out], ins)

        def tt(eng, out, a, b, op):
            h = nc.vector if eng == "dve" else nc.gpsimd
            P.op(eng, lambda: h.tensor_tensor(out.ap, a.ap, b.ap, op), [out], [a, b])

        def ts(eng, out, a, s1, op0, s2=None, op1=None):
            h = nc.vector if eng == "dve" else nc.gpsimd
            ins = [a] + [x for x in (s1, s2) if isinstance(x, V)]
            s1a = s1.ap if isinstance(s1, V) else s1
            s2a = s2.ap if isinstance(s2, V) else s2
            if op1 is None:
                P.op(eng, lambda: h.tensor_scalar(out.ap, a.ap, s1a, None, op0), [out], ins)
            else:
                P.op(eng, lambda: h.tensor_scalar(out.ap, a.ap, s1a, s2a, op0, op1), [out], ins)

        def stt(out, a, s, b, op0, op1):
            ins = [a, b] + ([s] if isinstance(s, V) else [])
            sa = s.ap if isinstance(s, V) else s
            P.op("dve", lambda: nc.vector.scalar_tensor_tensor(out.ap, a.ap, sa, b.ap, op0, op1), [out], ins)

        def cp(eng, out, in_):
            if eng == "act":
                P.op("act", lambda: nc.scalar.copy(out.ap, in_.ap), [out], [in_])
            else:
                h = nc.vector if eng == "dve" else nc.gpsimd
                P.op(eng, lambda: h.tensor_copy(out.ap, in_.ap), [out], [in_])

        def memset(eng, out, val):
            h = nc.vector if eng == "dve" else nc.gpsimd
            P.op(eng, lambda: h.memset(out.ap, val), [out], [])

        def recip(out, in_):
            P.op("dve", lambda: nc.vector.reciprocal(out.ap, in_.ap), [out], [in_])

        def ttr(out, a, b, accum):
            P.op("dve", lambda: nc.vector.scalar_tensor_tensor(out.ap, a.ap, 1.0, b.ap, ALU.mult, ALU.mult, accum_out=accum.ap), [out, accum], [a, b])

        def redx(out, in_):
            P.op("dve", lambda: nc.vector.tensor_reduce(out.ap, in_.ap, AX.X, ALU.add), [out], [in_])

        identf = sb("identf", [128, 128]); identb = sb("identb", [128, 128], BF16)
        cm = sb("cm", [128, 128]); cl = sb("cl", [128, 128]); su = sb("su", [128, 128])
        onesb = sb("onesb", [128, 4], BF16)
        Videntf = V(identf[:], "identf"); Videntb = V(identb[:], "identb")
        P.dma("sp", Videntf, V(c_ident, "c_ident"))
        P.dma("sp", V(cm[:], "cm"), V(c_cm, "c_cm"))
        P.dma("sp", V(cl[:], "cl"), V(c_cl, "c_cl"))
        P.dma("sp", V(su[:], "su"), V(c_su, "c_su"))
        cp("dve", Videntb, Videntf)
        memset("dve", V(onesb[:], "onesb"), 1.0)

        psA = [psum(f"psA{i}", [128, 512]) for i in range(2)]
        psT = psum("psT", [128, 1024], BF16)
        psN = psum("psN", [128, 512])
        psG = [psum(f"psG{i}", [128, 512]) for i in range(2)]
        psC = [psum(f"psC{i}", [128, 512]) for i in range(2)]

        with ExitStack() as st0:
            stg = Ring(nc, st0, "stg", [128, KC, 512], F32, 2)
            wbr = Ring(nc, st0, "wbr", [128, KC, 512], BF16, 2)
            cT = st0.enter_context(nc.sbuf_tensor("cT", [128, KC, 4], F32))
            modr = st0.enter_context(nc.sbuf_tensor("modr", [4, 3 * D], F32))
            nseq = 1 + NSMP
            memset("dve", V(cT[:], "cT"), 0.0)
            for s in range(nseq):
                P.dma("sp", V(cT[:, :, s], "cT"), V(c_all[s].rearrange("(k p) -> p k", p=128), "c_all"),
                      allow_slow_non_contiguous=True)
            cnt = 0
            for l in range(depth):
                for (src, dst, ncol) in ((w_in[l], wbf[l], INC), (w_proj_a[l], wpa[l], D), (w_proj_b[l], wpb[l], D), (w_out[l], wo[l], D)):
                    srcv = src.rearrange("(k p) n -> p k n", p=128)
                    dstv = dst.rearrange("(k p) n -> p k n", p=128)
                    for c0 in range(0, ncol, 512):
                        w = min(512, ncol - c0)
                        t32, k32 = stg.next()
                        t16, k16 = wbr.next()
                        P.dma("sp", V(t32[:, :, :w], k32), V(srcv[:, :, c0:c0 + w], "wsrc"))
                        eng = ("dve", "act", "pool")[cnt % 3]
                        cnt += 1
                        cp(eng, V(t16[:, :, :w], k16), V(t32[:, :, :w], k32))
                        P.dma("pool", V(dstv[:, :, c0:c0 + w], "D:w16"), V(t16[:, :, :w], k16))
                wav = w_ada[l].rearrange("(k p) n -> p k n", p=128)
                for s in range(nseq):
                    P.dma("sp", V(modr[s:s + 1, :], "modr"), V(b_ada[l:l + 1, :], "b_ada"))
                for jc in range(6):
                    t32, k32 = stg.next()
                    P.dma("sp", V(t32[:], k32), V(wav[:, :, jc * 512:(jc + 1) * 512], "wsrc"))
                    for kc in range(KC):
                        mm(V(psA[0][0:nseq, :], "psA0"), V(cT[:, kc, 0:nseq], "cT"), V(t32[:, kc, :], k32), start=(kc == 0), stop=(kc == KC - 1))
                    tt("dve", V(modr[0:nseq, jc * 512:(jc + 1) * 512], "modr"), V(modr[0:nseq, jc * 512:(jc + 1) * 512], "modr"),
                       V(psA[0][0:nseq, :], "psA0"), ALU.add)
                P.dma("pool", V(mods[l], "D:mods"), V(modr[0:nseq, :], "modr"))
            P.barrier_sp()

        slabs = Ring(nc, st, "slab", [128, KC, 512], BF16, 2)
        X = sb("X", [128, 2, D])
        Hb = sb("Hb", [128, D], BF16)
        tmpA = sb("tmpA", [128, D]); tmpB = sb("tmpB", [128, D])
        hT = sb("hT", [128, KC, 256], BF16)
        Uc = Ring(nc, st, "Uc", [128, 3 + 256], F32, 2)
        accr = Ring(nc, st, "accr", [128, 256], F32, 2)
        sqr = Ring(nc, st, "sqr", [128, 256], BF16, 2)
        HALO = sb("HALO", [128, 24, 3])
        QKV = sb("QKV", [128, 24, 256], BF16)
        QB = sb("QB", [128, 2, D], BF16); KBb = sb("KBb", [128, 2, D], BF16)
        stgf = Ring(nc, st, "stgf", [128, D], F32, 2)
        VB = sb("VB", [128, 2, H, 130], BF16)
        QT = sb("QT", [128, H, 256], BF16)
        KTt = sb("KTt", [128, H, 256], BF16)
        KTr = Ring(nc, st, "KTr", [128, 1024], BF16, 2)
        VAr = Ring(nc, st, "VAr", [128, 8, 130], BF16, 2)
        PTr = Ring(nc, st, "PTr", [128, 2, 256], BF16, 3)
        S = sb("S", [128, H, 128]); Sbf = sb("Sbf", [128, H, 128], BF16)
        OAn = sb("OAn", [128, 2, D]); OBn = sb("OBn", [128, 2, D])
        OA = sb("OA", [128, 2, D], BF16); OB = sb("OB", [128, 2, D], BF16)
        SGA = sb("SGA", [128, 2, D], BF16); SGB = sb("SGB", [128, 2, D], BF16)
        OAT = QT; OBT = KTt
        MG = OAn; MGb = sb("MGb", [128, 2, D], BF16)
        modB = sb("modB", [128, 3 * D]); nwB = sb("nwB", [128, D])
        cw = sb("cw", [128, 24, 4])
        alB = sb("alB", [128, H]); dtB = sb("dtB", [128, H]); gnwB = sb("gnwB", [128, 128]); dnwB = sb("dnwB", [128, 128])
        lqB = sb("lqB", [128, 256]); lamc = sb("lamc", [128, 4]); fnwB = sb("fnwB", [128, D])
        cosT = sb("cosT", [128, 2, 32]); sinT = sb("sinT", [128, 2, 32])
        sm = sb("sm", [128, 256])
        gf = Ring(nc, st, "gf", [128, 128], F32, 2)
        gfn = {}

        def KO(nm, n, hs=None):
            hh = range(H) if hs is None else range(hs * 4, hs * 4 + 4)
            return [(nm, n, h_) for h_ in hh]
        TA = [("tmpA", 0), ("tmpA", 1)]; TB = [("tmpB", 0), ("tmpB", 1)]
        SK = [("S", h_) for h_ in range(H)]; SBK = [("Sbf", h_) for h_ in range(H)]
        HK = [("HALO", c_) for c_ in range(24)]

        def G32(tag, n=2):
            if tag not in gfn:
                gfn[tag] = Ring(nc, st, "g_" + tag, [128, 128], F32, n)
            return gfn[tag].next()

        gbn = {}

        def G16(tag, n=2):
            if tag not in gbn:
                gbn[tag] = Ring(nc, st, "b_" + tag, [128, 128], BF16, n)
            return gbn[tag].next()

        P.dma("sp", V(fnwB[:], "fnwB"), V(final_norm_w.partition_broadcast(128).rearrange("p o d -> p (o d)"), "fnw"))
        memset("pool", V(VB[:, :, :, 128:130], "VBones"), 1.0)

        def load_slab(wd, c0, w):
            t, k = slabs.next()
            P.dma("sp", V(t[:, :, :w], k), V(wd.rearrange("(k p) n -> p k n", p=128)[:, :, c0:c0 + w], "D:w16"))
            return t, k

        cpi = [0]

        def evac_eng():
            cpi[0] += 1
            return ("act", "dve")[cpi[0] % 2]

        try:
          stop_at(0)
          for l in range(depth):
              lam_init = lam_inits[l]
              for j in range(4):
                  P.dma("sp", V(cw[:, :, j], "cw"), V(conv_w[l, j].rearrange("(c p) -> p c", p=128), "conv_w"), allow_slow_non_contiguous=True)
              P.dma("sp", V(alB[:], "alB"), V(a_log[l:l + 1, :].partition_broadcast(128).rearrange("p o d -> p (o d)"), "a_log"))
              P.dma("sp", V(dtB[:], "dtB"), V(dt_bias[l:l + 1, :].partition_broadcast(128).rearrange("p o d -> p (o d)"), "dt_bias"))
              P.dma("sp", V(gnwB[:], "gnwB"), V(gdn_norm_w[l:l + 1, :].partition_broadcast(128).rearrange("p o d -> p (o d)"), "gnw"))
              P.dma("sp", V(dnwB[:], "dnwB"), V(diff_norm_w[l:l + 1, :].partition_broadcast(128).rearrange("p o d -> p (o d)"), "dnw"))
              P.dma("sp", V(lqB[:], "lqB"), V(lam_qk[l:l + 1, :].partition_broadcast(128).rearrange("p o d -> p (o d)"), "lam_qk"))
              P.dma("sp", V(nwB[:], "nwB"), V(norm_w[l:l + 1, :].partition_broadcast(128).rearrange("p o d -> p (o d)"), "norm_w"))
              act(V(alB[:], "alB"), V(alB[:], "alB"), AF.Exp)
              ts("dve", V(alB[:], "alB"), V(alB[:], "alB"), -1.0, ALU.mult)
              ts("dve", V(dnwB[:], "dnwB"), V(dnwB[:], "dnwB"), 1.0 - lam_init, ALU.mult)
              t0, k0 = G32("j")
              ttr(V(t0[:, 0:64], k0), V(lqB[:, 0:64], "lqB"), V(lqB[:, 64:128], "lqB"), V(lamc[:, 0:1], "lamc"))
              t0, k0 = G32("j")
              ttr(V(t0[:, 0:64], k0), V(lqB[:, 128:192], "lqB"), V(lqB[:, 192:256], "lqB"), V(lamc[:, 1:2], "lamc"))
              act(V(lamc[:, 0:2], "lamc"), V(lamc[:, 0:2], "lamc"), AF.Exp)
              tt("dve", V(lamc[:, 2:3], "lamc"), V(lamc[:, 1:2], "lamc"), V(lamc[:, 0:1], "lamc"), ALU.subtract)
              ts("dve", V(lamc[:, 2:3], "lamc"), V(lamc[:, 2:3], "lamc"), -lam_init, ALU.add)

              for sq in seqs:
                  T, TT, TP, L, PA = sq["T"], sq["TT"], sq["TP"], sq["L"], sq["P"]
                  NS = TT // TP
                  NCH = TP // L
                  nlev = 5 if L == 64 else 4
                  ntile = T // TT
                  last_layer = (l == depth - 1)
                  x_src = sq["x"] if l == 0 else sq["xres"][(l - 1) % 2]
                  x_dst = sq["xres"][l % 2]
                  if sq["s0"] is None:
                      memset("pool", V(S[:], SK), 0.0)
                      memset("pool", V(HALO[:], HK), 0.0)
                  else:
                      P.dma("sp", V(S[:], SK), V(sq["s0"](l).rearrange("h k v -> k h v"), "state_ssm"))
                      t0, k0 = modB, "modB"
                      P.dma("sp", V(t0[0:3, :], k0), V(sq["cv0"](l), "state_conv"))
                      for c in range(24):
                          tr(V(psN[:, c * 4:c * 4 + 3], "psN"), V(t0[0:3, c * 128:(c + 1) * 128], k0), V(identf[0:3, 0:3], "identf"))
                      cp("dve", V(HALO[:], HK), V(psN[:, 0:96].rearrange("p (c j) -> p c j", j=4)[:, :, 0:3], "psN"))
                  P.dma("sp", V(modB[:], "modB"), V(mods[l, sq["cidx"]:sq["cidx"] + 1, :].partition_broadcast(128).rearrange("p o d -> p (o d)"), "D:mods"))
                  stt(V(modB[:, D:2 * D], "modB"), V(modB[:, D:2 * D], "modB"), 1.0, V(nwB[:], "nwB"), ALU.add, ALU.mult)
                  shiftB = V(modB[:, 0:D], "modB"); wmodB = V(modB[:, D:2 * D], "modB")
                  cp("act", V(Sbf[:], SBK), V(S[:], SK))

                  for it in range(ntile):
                      tok0 = it * TT
                      P.dma("sp", V(X[:TP, :NS, :], "X"), V(x_src[tok0:tok0 + TT, :].rearrange("(n p) d -> p n d", p=TP), "D:xres" + sq["name"]))
                      P.dma("sp", V(cosT[:TP, :NS, :], "cosT"), V(sq["cos"][tok0:tok0 + TT, :].rearrange("(n p) d -> p n d", p=TP), "cos"))
                      P.dma("sp", V(sinT[:TP, :NS, :], "sinT"), V(sq["sin"][tok0:tok0 + TT, :].rearrange("(n p) d -> p n d", p=TP), "sin"))
                      for n in range(NS):
                          ttr(V(tmpA[:TP, :], TA), V(X[:TP, n, :], "X"), V(X[:TP, n, :], "X"), V(sm[:TP, n:n + 1], "sm_ss"))
                      ts("dve", V(sm[:TP, 0:NS], "sm_ss"), V(sm[:TP, 0:NS], "sm_ss"), 1.0 / D, ALU.mult, EPS, ALU.add)
                      act(V(sm[:TP, 0:NS], "sm_ss"), V(sm[:TP, 0:NS], "sm_ss"), AF.Sqrt)
                      recip(V(sm[:TP, 2:2 + NS], "sm_rs"), V(sm[:TP, 0:NS], "sm_ss"))
                      for n in range(NS):
                          stt(V(tmpA[:TP, :], TA), V(X[:TP, n, :], "X"), V(sm[:TP, 2 + n:3 + n], "sm_rs"), V(modB[:TP, D:2 * D], "modB"), ALU.mult, ALU.mult)
                          tt("pool", V(Hb[:TP, :], "Hb"), V(tmpA[:TP, :], TA), V(modB[:TP, 0:D], "modB"), ALU.add)
                          for g in range(2):
                              for j in range(4):
                                  kc = 4 * g + j
                                  tr(V(psT[:, g * 512 + j * 128:g * 512 + j * 128 + TP], ("psT", g)), V(Hb[:TP, kc * 128:(kc + 1) * 128], "Hb"), V(identb[:TP, :TP], "identb"))
                              cp(evac_eng(), V(hT[:, 4 * g:4 * g + 4, n * TP:(n + 1) * TP], "hT"),
                                 V(psT[:, g * 512:(g + 1) * 512].rearrange("p (j t) -> p j t", t=128)[:, :, :TP], ("psT", g)))

                      stop_at(1)
                      for s6 in range(6):
                          t, k = load_slab(wbf[l], s6 * 512, 512)
                          for cc in range(4):
                              c = 4 * s6 + cc
                              pa = psA[c % 2]; pk = f"psA{c % 2}"
                              for kc in range(KC):
                                  mm(V(pa[:, 0:TT], pk), V(t[:, kc, cc * 128:(cc + 1) * 128], k), V(hT[:, kc, 0:TT], "hT"), start=(kc == 0), stop=(kc == KC - 1))
                              u, uk = Uc.next()
                              cp("act", V(u[:, 3:3 + TT], uk), V(pa[:, 0:TT], pk))
                              cp("pool", V(u[:, 0:3], uk), V(HALO[:, c, :], ("HALO", c)))
                              a, ak = accr.next()
                              ts("dve", V(a[:, :TT], ak), V(u[:, 0:TT], uk), V(cw[:, c, 0:1], "cw"), ALU.mult)
                              for j in range(1, 4):
                                  stt(V(a[:, :TT], ak), V(u[:, j:j + TT], uk), V(cw[:, c, j:j + 1], "cw"), V(a[:, :TT], ak), ALU.mult, ALU.add)
                              cp("pool", V(HALO[:, c, :], ("HALO", c)), V(u[:, TT:TT + 3], uk))
                              act(V(QKV[:, c, :TT], ("QKV", c)), V(a[:, :TT], ak), AF.Silu)
                              if c < 16:
                                  q2, qk2 = sqr.next()
                                  act(V(q2[:, :TT], qk2), V(QKV[:, c, :TT], ("QKV", c)), AF.Square)
                                  for n in range(NS):
                                      mm(V(psN[:TP, 128 + n * 16 + c:128 + n * 16 + c + 1], "psN"), V(q2[:, n * TP:(n + 1) * TP], qk2), V(onesb[:, 0:1], "onesb"))
                      if it == ntile - 1:
                          nl = NS - 1
                          for s6 in range(6):
                              t, k = load_slab(wbf[l], s6 * 512, 512)
                              for kc in range(KC):
                                  mm(V(psA[0][:TP, :], "psA0"), V(hT[:, kc, nl * TP:(nl + 1) * TP], "hT"), V(t[:, kc, :], k), start=(kc == 0), stop=(kc == KC - 1))
                              pb = 64 if TP == 128 else 0
                              cp("act", V(tmpB[pb:TP, (s6 % 2) * 512:(s6 % 2) * 512 + 512], ("tmpB", s6 % 2)), V(psA[0][pb:TP, :], "psA0"))
                              P.dma("pool", V(sq["co"](l)[:, s6 * 512:(s6 + 1) * 512], "conv_out"), V(tmpB[TP - 3:TP, (s6 % 2) * 512:(s6 % 2) * 512 + 512], ("tmpB", s6 % 2)))

                      stop_at(2)
                      t, k = load_slab(wbf[l], C_BA, 16)
                      for n in range(NS):
                          for kc in range(KC):
                              mm(V(psN[:TP, 200 + n * 16:200 + n * 16 + 16], "psN"), V(hT[:, kc, n * TP:(n + 1) * TP], "hT"), V(t[:, kc, 0:16], k), start=(kc == 0), stop=(kc == KC - 1))
                      for n in range(NS):
                          bcol = V(sm[:TP, 8 + n * 8:16 + n * 8], "sm_beta"); gcol = V(sm[:TP, 24 + n * 8:32 + n * 8], "sm_g")
                          act(bcol, V(psN[:TP, 200 + n * 16:208 + n * 16], "psN"), AF.Sigmoid)
                          t1 = V(sm[:TP, 40:48], "sm_t1"); t2 = V(sm[:TP, 48:56], "sm_t2")
                          tt("dve", t1, V(psN[:TP, 208 + n * 16:216 + n * 16], "psN"), V(dtB[:TP, :], "dtB"), ALU.add)
                          stt(t2, t1, -1.0, t1, ALU.mult, ALU.min)
                          act(t2, t2, AF.Exp)
                          ts("dve", t2, t2, 1.0, ALU.add)
                          act(t2, t2, AF.Ln)
                          stt(t2, t1, 0.0, t2, ALU.max, ALU.add)
                          tt("dve", gcol, t2, V(alB[:TP, :], "alB"), ALU.mult)

                      stop_at(3)
                      for gi, (cbase, kind) in enumerate(((C_QB, "q"), (C_KB, "k"), (C_VB, "v"))):
                          for hs in range(2):
                              t, k = load_slab(wbf[l], cbase + hs * 512, 512)
                              for n in range(NS):
                                  pa = psA[(n + hs) % 2]; pk = f"psA{(n + hs) % 2}"
                                  for kc in range(KC):
                                      mm(V(pa[:TP, :], pk), V(hT[:, kc, n * TP:(n + 1) * TP], "hT"), V(t[:, kc, :], k), start=(kc == 0), stop=(kc == KC - 1))
                                  cols = slice(hs * 512, (hs + 1) * 512)
                                  if kind == "v":
                                      if hs == 0:
                                          sq["_vf%d" % n] = stgf.next()
                                      sf, sfk = sq["_vf%d" % n]
                                      cp("act", V(sf[:TP, cols], (sfk, hs)), V(pa[:TP, :], pk))
                                      cp("pool", V(VB[:TP, n, hs * 4:(hs + 1) * 4, 0:128], ("VB", n, hs)), V(sf[:TP, cols].rearrange("p (h d) -> p h d", d=128), (sfk, hs)))
                                      if hs == 1:
                                          P.dma("pool", V(sq["vo"](l)[:, tok0 + n * TP:tok0 + (n + 1) * TP, :].rearrange("h p d -> p h d"), "v_out"),
                                                V(sf[:TP, :].rearrange("p (h d) -> p h d", d=128), [(sfk, 0), (sfk, 1)]))
                                  else:
                                      pv = pa[:TP, :].rearrange("p (a h d) -> p a h d", h=2, d=32)
                                      x1 = V(pv[:, :, 0, :], pk); x2 = V(pv[:, :, 1, :], pk)
                                      cB = V(cosT[:TP, n:n + 1, :].broadcast_to([TP, 8, 32]), "cosT")
                                      sB = V(sinT[:TP, n:n + 1, :].broadcast_to([TP, 8, 32]), "sinT")
                                      ta = V(tmpA[:TP, 0:256].rearrange("p (a d) -> p a d", d=32), TA)
                                      tb = V(tmpA[:TP, 256:512].rearrange("p (a d) -> p a d", d=32), TA)
                                      tc = V(tmpA[:TP, 512:768].rearrange("p (a d) -> p a d", d=32), TA)
                                      td = V(tmpA[:TP, 768:1024].rearrange("p (a d) -> p a d", d=32), TA)
                                      tt("dve", ta, x1, cB, ALU.mult)
                                      tt("dve", tb, x2, sB, ALU.mult)
                                      tt("dve", tc, x1, sB, ALU.mult)
                                      tt("dve", td, x2, cB, ALU.mult)
                                      if kind == "q":
                                          ov = QB[:TP, n, cols].rearrange("p (a h d) -> p a h d", h=2, d=32)
                                          tt("pool", V(ov[:, :, 0, :], ("QB", n, hs)), ta, tb, ALU.subtract)
                                          tt("pool", V(ov[:, :, 1, :], ("QB", n, hs)), tc, td, ALU.add)
                                      else:
                                          if hs == 0:
                                              sq["_kf%d" % n] = stgf.next()
                                          sf, sfk = sq["_kf%d" % n]
                                          ov = sf[:TP, cols].rearrange("p (a h d) -> p a h d", h=2, d=32)
                                          tt("pool", V(ov[:, :, 0, :], (sfk, hs)), ta, tb, ALU.subtract)
                                          tt("pool", V(ov[:, :, 1, :], (sfk, hs)), tc, td, ALU.add)
                                          cp("act", V(KBb[:TP, n, cols], ("KBb", n, hs)), V(sf[:TP, cols], (sfk, hs)))
                                          if hs == 1:
                                              P.dma("pool", V(sq["ko"](l)[:, tok0 + n * TP:tok0 + (n + 1) * TP, :].rearrange("h p d -> p h d"), "k_out"),
                                                    V(sf[:TP, :].rearrange("p (h d) -> p h d", d=128), [(sfk, 0), (sfk, 1)]))
                      for n in range(NS):
                          for (src, sname, dst, dname) in ((QB, "QB", QT, "QT"), (KBb, "KBb", KTt, "KTt")):
                              for g in range(2):
                                  for j in range(4):
                                      hh = 4 * g + j
                                      tr(V(psT[:, g * 512 + j * 128:g * 512 + j * 128 + TP], ("psT", g)), V(src[:TP, n, hh * 128:(hh + 1) * 128], (sname, n, g)), V(identb[:TP, :TP], "identb"))
                                  cp(evac_eng(), V(dst[:, 4 * g:4 * g + 4, n * TP:(n + 1) * TP], dname),
                                     V(psT[:, g * 512:(g + 1) * 512].rearrange("p (j t) -> p j t", t=128)[:, :, :TP], ("psT", g)))
                          P.dma("pool", V(sq["vas"][tok0 + n * TP:tok0 + (n + 1) * TP, :, :], "D:vas" + sq["name"]), V(VB[:TP, n, :, :], [("VB", n, 0), ("VB", n, 1), "VBones"]))
                      P.dma("pool", V(sq["kts"][:, :, tok0:tok0 + TT].rearrange("h p t -> p h t"), "D:kts" + sq["name"]), V(KTt[:, :, :TT], "KTt"))

                      stop_at(4)
                      for n in range(NS):
                          bsl = slice(n * TP, (n + 1) * TP)
                          beta = sm[:TP, 8 + n * 8:16 + n * 8]; gcolap = sm[:TP, 24 + n * 8:32 + n * 8]
                          ik = sm[:TP, 64:72]; rk = sm[:TP, 72:80]; rq = sm[:TP, 80:88]; Gc = sm[:TP, 88:96]; GL = sm[:TP, 96:104]
                          nEG = sm[:TP, 104:112]; kds = sm[:TP, 112:120]; brk2 = sm[:TP, 120:128]; brk = sm[:TP, 128:136]; tq = sm[:TP, 136:144]
                          K = "sm_g2"
                          ts("dve", V(tq, K), V(psN[:TP, 128 + n * 16:136 + n * 16], "psN"), EPS, ALU.add)
                          act(V(tq, K), V(tq, K), AF.Sqrt)
                          recip(V(rq, K), V(tq, K))
                          ts("dve", V(rq, K), V(rq, K), 128 ** -0.5, ALU.mult)
                          ts("dve", V(ik, K), V(psN[:TP, 136 + n * 16:144 + n * 16], "psN"), EPS, ALU.add)
                          act(V(ik, K), V(ik, K), AF.Sqrt)
                          recip(V(rk, K), V(ik, K))
                          mm(V(psN[:TP, 300:308], "psN"), V(cm[:TP, :TP], "cm"), V(gcolap, "sm_g"))
                          mm(V(psN[:TP, 308:316], "psN"), V(cl[:TP, :TP], "cl"), V(gcolap, "sm_g"))
                          cp("dve", V(sm[:TP, 88:104], K), V(psN[:TP, 300:316], "psN"))
                          act(V(nEG, K), V(Gc, K), AF.Exp)
                          ts("dve", V(nEG, K), V(nEG, K), -1.0, ALU.mult)
                          tt("dve", V(kds, K), V(GL, K), V(Gc, K), ALU.subtract)
                          act(V(kds, K), V(kds, K), AF.Exp)
                          tt("dve", V(kds, K), V(kds, K), V(rk, K), ALU.mult)
                          tt("dve", V(brk, K), V(beta, "sm_beta"), V(rk, K), ALU.mult)
                          tt("dve", V(brk2, K), V(brk, K), V(rk, K), ALU.mult)
                          for h in range(H):
                              qTh = V(QKV[:, h, bsl], ("QKV", h)); kTh = V(QKV[:, 8 + h, bsl], ("QKV", 8 + h)); vTh = V(QKV[:, 16 + h, bsl], ("QKV", 16 + h))
                              pg = psG[h % 2]; pgk = f"psG{h % 2}"
                              pc = psC[h % 2]; pck = f"psC{h % 2}"
                              grep, grk = G32("grep")
                              cp("pool", V(grep[:TP, :], grk), V(sm[:TP, 24 + n * 8 + h:25 + n * 8 + h].broadcast_to([TP, 128]), "sm_g"))
                              mm(V(pg[:, 0:TP], (pgk, 0)), V(grep[:TP, :], grk), V(cm[:TP, :TP], "cm"))
                              EB, EBk = G32("EB")
                              act(V(EB[:, :TP], EBk), V(pg[:, 0:TP], (pgk, 0)), AF.Exp)
                              DT, DTk = G32("DT")
                              ts("dve", V(DT[:TP, :TP], DTk), V(pg[:TP, 0:TP], (pgk, 0)), V(sm[:TP, 88 + h:89 + h], K), ALU.subtract, 0.0, ALU.min)
                              act(V(DT[:TP, :TP], DTk), V(DT[:TP, :TP], DTk), AF.Exp)
                              mm(V(pg[:TP, 128:128 + TP], (pgk, 1)), kTh, kTh)
                              mm(V(pg[:TP, 256:256 + TP], (pgk, 2)), kTh, qTh)
                              Vm, Vmk = G32("Vm")
                              stt(V(Vm[:TP, :TP], Vmk), V(pg[:TP, 128:128 + TP], (pgk, 1)), V(sm[:TP, 120 + h:121 + h], K), V(DT[:TP, :TP], DTk), ALU.mult, ALU.mult)
                              tt("pool", V(Vm[:TP, :TP], Vmk), V(Vm[:TP, :TP], Vmk), V(su[:TP, :TP], "su"), ALU.mult)
                              QKm, QKmk = G16("QKm")
                              q32, q32k = G32("q32")
                              stt(V(q32[:TP, :TP], q32k), V(pg[:TP, 256:256 + TP], (pgk, 2)), V(sm[:TP, 72 + h:73 + h], K), V(DT[:TP, :TP], DTk), ALU.mult, ALU.mult)
                              tt("pool", V(QKm[:TP, :TP], QKmk), V(q32[:TP, :TP], q32k), V(cm[:TP, :TP], "cm"), ALU.mult)
                              R, Rk = G32("R", 3)
                              stt(V(R[:TP, :TP], Rk), V(Vm[:TP, :TP], Vmk), -1.0, V(identf[:TP, :TP], "identf"), ALU.mult, ALU.add)
                              tr(V(pg[:TP, 384:384 + TP], (pgk, 3)), V(Vm[:TP, :TP], Vmk), V(identf[:TP, :TP], "identf"))
                              PT_, PTk = G32("PT", 3)
                              cp("act", V(PT_[:TP, :TP], PTk), V(pg[:TP, 384:384 + TP], (pgk, 3)))
                              Pm, Pmk = Vm, Vmk
                              for lev in range(nlev):
                                  mm(V(pg[:TP, 128:128 + TP], (pgk, 1)), V(Pm[:TP, :TP], Pmk), V(PT_[:TP, :TP], PTk))
                                  if lev < nlev - 1:
                                      mm(V(pg[:TP, 256:256 + TP], (pgk, 2)), V(PT_[:TP, :TP], PTk), V(Pm[:TP, :TP], Pmk))
                                  PTn, PTnk = G32("PT", 3)
                                  cp("act", V(PTn[:TP, :TP], PTnk), V(pg[:TP, 128:128 + TP], (pgk, 1)))
                                  if lev < nlev - 1:
                                      Pn, Pnk = G32("Pm", 3)
                                      cp("act", V(Pn[:TP, :TP], Pnk), V(pg[:TP, 256:256 + TP], (pgk, 2)))
                                  mm(V(pg[:TP, 384:384 + TP], (pgk, 3)), V(PTn[:TP, :TP], PTnk), V(R[:TP, :TP], Rk))
                                  Rn, Rnk = G32("R", 3)
                                  tt("dve", V(Rn[:TP, :TP], Rnk), V(R[:TP, :TP], Rk), V(pg[:TP, 384:384 + TP], (pgk, 3)), ALU.add)
                                  R, Rk = Rn, Rnk
                                  PT_, PTk = PTn, PTnk
                                  if lev < nlev - 1:
                                      Pm, Pmk = Pn, Pnk
                              Rb, Rbk = G16("Rb")
                              cp("pool", V(Rb[:TP, :TP], Rbk), V(R[:TP, :TP], Rk))
                              qdT, qdk = G16("qdT")
                              tt("dve", V(qdT[:, :TP], qdk), qTh, V(EB[:, :TP], EBk), ALU.mult)
                              tr(V(psT[:TP, 0:128], ("psT", 0)), vTh, Videntb)
                              vs_, vsk = G16("vs")
                              ts("dve", V(vs_[:TP, :], vsk), V(psT[:TP, 0:128], ("psT", 0)), V(sm[:TP, 64 + h:65 + h], K), ALU.mult)
                              tr(V(psT[:TP, 512:640], ("psT", 1)), kTh, Videntb)
                              kd, kdk = G16("kd")
                              ts("dve", V(kd[:TP, :], kdk), V(psT[:TP, 512:640], ("psT", 1)), V(sm[:TP, 112 + h:113 + h], K), ALU.mult)
                              U2, U2k = G16("U2"); W_, Wk = G16("W")
                              for c in range(NCH):
                                  r = slice(c * L, (c + 1) * L)
                                  Sh = V(Sbf[:, h, :], ("Sbf", h))
                                  mm(V(pc[:TP, 0:128], (pck, 0)), kTh, Sh)
                                  stt(V(U2[r, :], U2k), V(pc[r, 0:128], (pck, 0)), V(sm[r, 104 + h:105 + h], K), V(vs_[r, :], vsk), ALU.mult, ALU.add)
                                  mm(V(pc[:TP, 128:256], (pck, 1)), V(Rb[r, :TP], Rbk), V(U2[r, :], U2k))
                                  ts("dve", V(W_[r, :], Wk), V(pc[r, 128:256], (pck, 1)), V(sm[r, 128 + h:129 + h], K), ALU.mult)
                                  mm(V(pc[:TP, 256:384], (pck, 2)), V(qdT[:, :TP], qdk), Sh, start=True, stop=False)
                                  mm(V(pc[:TP, 256:384], (pck, 2)), V(QKm[r, :TP], QKmk), V(W_[r, :], Wk), start=False, stop=True)
                                  act(V(OAn[r, n, h * 128:(h + 1) * 128], ("OAn", n, h)), V(pc[r, 256:384], (pck, 2)), AF.Identity, scale=V(sm[r, 80 + h:81 + h], K))
                                  mm(V(pc[:, 384:512], (pck, 3)), V(kd[r, :], kdk), V(W_[r, :], Wk))
                                  stt(V(S[:, h, :], ("S", h)), V(S[:, h, :], ("S", h)), V(EB[:, (c + 1) * L - 1:(c + 1) * L], EBk), V(pc[:, 384:512], (pck, 3)), ALU.mult, ALU.add)
                                  cp("act", V(Sbf[:, h, :], ("Sbf", h)), V(S[:, h, :], ("S", h)))
                          okeys = [("OAn", n, h) for h in range(H)]
                          tt("pool", V(tmpB[:TP, :], TB), V(OAn[:TP, n, :], okeys), V(OAn[:TP, n, :], okeys), ALU.mult)
                          redx(V(sm[:TP, 144:152], "sm_on"), V(tmpB[:TP, :].rearrange("p (h d) -> p h d", d=128), TB))
                          ts("dve", V(sm[:TP, 144:152], "sm_on"), V(sm[:TP, 144:152], "sm_on"), 1.0 / 128, ALU.mult, EPS, ALU.add)
                          act(V(sm[:TP, 144:152], "sm_on"), V(sm[:TP, 144:152], "sm_on"), AF.Sqrt)
                          recip(V(sm[:TP, 152:160], "sm_on2"), V(sm[:TP, 144:152], "sm_on"))
                          o3 = OAn[:TP, n, :].rearrange("p (h d) -> p h d", d=128)
                          tt("dve", V(o3, okeys), V(o3, okeys), V(sm[:TP, 152:160].unsqueeze(2).broadcast_to([TP, 8, 128]), "sm_on2"), ALU.mult)
                          tt("pool", V(o3, okeys), V(o3, okeys), V(gnwB[:TP, :].unsqueeze(1).broadcast_to([TP, 8, 128]), "gnwB"), ALU.mult)
                      if it == ntile - 1:
                          P.dma("pool", V(sq["so"](l).rearrange("h k v -> k h v"), "ssm_out"), V(S[:], SK))

                      stop_at(5)
                      nkt_new = (tok0 + TT) // TP if PA == 0 else 1
                      for h in range(H):
                          loaders = []
                          if PA > 0:
                              for p0 in range(0, PA, 1024):
                                  def ld_cache(p0=p0):
                                      npk = min(1024, PA - p0)
                                      nt8 = npk // 128
                                      kt_t, kt_k = KTr.next(); va_t, va_k = VAr.next()
                                      c1, c1k = tmpA[:].rearrange("p (t d) -> p t d", d=128), TA
                                      P.dma("sp", V(c1[:, :nt8, :], c1k), V(sq["ck"](l)[h, p0:p0 + npk, :].rearrange("(t p) d -> p t d", p=128), "cache_k"))
                                      kb16, kb16k = G16("kc16", 2)
                                      for t8 in range(nt8):
                                          cp("pool", V(kb16[:, :], kb16k), V(c1[:, t8, :], c1k))
                                          tr(V(psT[:, (t8 % 2) * 512:(t8 % 2) * 512 + 128], ("psT", t8 % 2)), V(kb16[:, :], kb16k), Videntb)
                                          cp(evac_eng(), V(kt_t[:, t8 * 128:(t8 + 1) * 128], kt_k), V(psT[:, (t8 % 2) * 512:(t8 % 2) * 512 + 128], ("psT", t8 % 2)))
                                          kb16, kb16k = G16("kc16", 2)
                                      c2, c2k = tmpB[:].rearrange("p (t d) -> p t d", d=128), TB
                                      P.dma("sp", V(c2[:, :nt8, :], c2k), V(sq["cvv"](l)[h, p0:p0 + npk, :].rearrange("(t p) d -> p t d", p=128), "cache_v"))
                                      cp("pool", V(va_t[:, :nt8, 0:128], va_k), V(c2[:, :nt8, :], c2k))
                                      memset("pool", V(va_t[:, :nt8, 128:130], va_k), 1.0)
                                      return (kt_t, kt_k, va_t, va_k, [(128, None)] * nt8)
                                  loaders.append(ld_cache)

                              def ld_new():
                                  kt_t, kt_k = KTr.next(); va_t, va_k = VAr.next()
                                  P.dma("sp", V(kt_t[:, :TS], kt_k), V(sq["kts"][h, :, :], "D:kts" + sq["name"]))
                                  P.dma("sp", V(va_t[:TS, 0, :], va_k), V(sq["vas"][:, h, :], "D:vas" + sq["name"]))
                                  return (kt_t, kt_k, va_t, va_k, [(TS, None)])
                              loaders.append(ld_new)
                          else:
                              nkeys = tok0 + TT
                              for p0 in range(0, nkeys, 1024):
                                  def ld_self(p0=p0):
                                      npk = min(1024, nkeys - p0)
                                      nt8 = npk // 128
                                      kt_t, kt_k = KTr.next(); va_t, va_k = VAr.next()
                                      P.dma("sp", V(kt_t[:, :npk], kt_k), V(sq["kts"][h, :, p0:p0 + npk], "D:kts" + sq["name"]))
                                      P.dma("sp", V(va_t[:, :nt8, :], va_k), V(sq["vas"][p0:p0 + npk, h, :].rearrange("(t p) d -> p t d", p=128), "D:vas" + sq["name"]))
                                      return (kt_t, kt_k, va_t, va_k, [(128, (p0 // 128) + j) for j in range(nt8)])
                                  loaders.append(ld_self)
                          npieces = len(loaders)
                          stop_at(50)
                          ngrp = NS
                          acc_k = lambda g, m: ("psCacc", g, m)
                          first = [True] * ngrp
                          nxt_piece = loaders[0]()
                          for pi_ in range(npieces):
                              (kt_t, kt_k, va_t, va_k, tiles) = nxt_piece
                              if pi_ + 1 < npieces:
                                  nxt_piece = loaders[pi_ + 1]()
                              for j, (nk, gkt) in enumerate(tiles):
                                  if gkt is None:
                                      g0 = 0
                                  else:
                                      g0 = max(0, gkt - (tok0 // TP))
                                      if g0 >= ngrp:
                                          continue
                                  q0 = g0 * TP
                                  ncol = TT - q0
                                  sbk = [(psG[0], "psG0"), (psG[1], "psG1")] if j % 2 == 0 else [(psA[0], "psA0"), (psA[1], "psA1")]
                                  stop_at(51)
                                  pt, ptk = PTr.next()
                                  for m in range(2):
                                      stb, stk = sbk[m]
                                      mm(V(stb[:nk, q0:TT], stk), V(kt_t[m * 64:(m + 1) * 64, j * 128:j * 128 + nk], kt_k),
                                         V(QT[m * 64:(m + 1) * 64, h, q0:TT], "QT"))
                                  for m in range(2):
                                      stb, stk = sbk[m]
                                      act(V(pt[:nk, m, q0:TT], ptk), V(stb[:nk, q0:TT], stk), AF.Exp, scale=0.125)
                                  if gkt is not None and gkt >= tok0 // TP:
                                      memset("pool", V(pt[64:128, :, q0:q0 + 64], ptk), 0.0)
                                  stop_at(52)
                                  for g in range(g0, ngrp):
                                      lastk = (gkt is None and (pi_ == npieces - 1) and (j == len(tiles) - 1)) or (gkt is not None and gkt == tok0 // TP + g)
                                      for m in range(2):
                                          mm(V(psC[g][:TP, m * 256:m * 256 + 129], [(f"psC{g}", 2 * m), (f"psC{g}", 2 * m + 1)]), V(pt[:nk, m, g * TP:(g + 1) * TP], ptk), V(va_t[:nk, j, 0:129], va_k),
                                             start=(first[g] and m == 0), stop=lastk)
                                      first[g] = False
                          stop_at(53)
                          for g in range(ngrp):
                              dn = V(sm[:TP, 160:162], "sm_dn")
                              cp("dve", V(sm[:TP, 160:161], "sm_dn"), V(psC[g][:TP, 128:129], [(f"psC{g}", 0), (f"psC{g}", 1)]))
                              cp("dve", V(sm[:TP, 161:162], "sm_dn"), V(psC[g][:TP, 256 + 128:256 + 129], [(f"psC{g}", 2), (f"psC{g}", 3)]))
                              recip(V(sm[:TP, 162:164], ["sm_dr", "sm_dr2"]), dn)
                              tt("dve", V(sm[:TP, 163:164], "sm_dr2"), V(sm[:TP, 163:164], "sm_dr2"), V(lamc[:TP, 2:3], "lamc"), ALU.mult)
                              t1, t1k = G32("att1")
                              act(V(t1[:TP, :], t1k), V(psC[g][:TP, 0:128], [(f"psC{g}", 0), (f"psC{g}", 1)]), AF.Identity, scale=V(sm[:TP, 162:163], "sm_dr"))
                              stt(V(OBn[:TP, g, h * 128:(h + 1) * 128], ("OBn", g, h)), V(psC[g][:TP, 256:384], [(f"psC{g}", 2), (f"psC{g}", 3)]), V(sm[:TP, 163:164], "sm_dr2"),
                                  V(t1[:TP, :], t1k), ALU.mult, ALU.add)
                      stop_at(54)
                      for n in range(NS):
                          okeys = [("OBn", n, h) for h in range(H)]
                          tt("pool", V(tmpB[:TP, :], TB), V(OBn[:TP, n, :], okeys), V(OBn[:TP, n, :], okeys), ALU.mult)
                          redx(V(sm[:TP, 144:152], "sm_on"), V(tmpB[:TP, :].rearrange("p (h d) -> p h d", d=128), TB))
                          ts("dve", V(sm[:TP, 144:152], "sm_on"), V(sm[:TP, 144:152], "sm_on"), 1.0 / 128, ALU.mult, EPS, ALU.add)
                          act(V(sm[:TP, 144:152], "sm_on"), V(sm[:TP, 144:152], "sm_on"), AF.Sqrt)
                          recip(V(sm[:TP, 152:160], "sm_on2"), V(sm[:TP, 144:152], "sm_on"))
                          o3 = OBn[:TP, n, :].rearrange("p (h d) -> p h d", d=128)
                          tt("dve", V(o3, okeys), V(o3, okeys), V(sm[:TP, 152:160].unsqueeze(2).broadcast_to([TP, 8, 128]), "sm_on2"), ALU.mult)
                          tt("pool", V(o3, okeys), V(o3, okeys), V(dnwB[:TP, :].unsqueeze(1).broadcast_to([TP, 8, 128]), "dnwB"), ALU.mult)

                      stop_at(6)
                      for (cbase, kind) in ((C_ZA, "za"), (C_ZB, "zb"), (C_GA, "ga"), (C_GB, "gb")):
                          for hs in range(2):
                              t, k = load_slab(wbf[l], cbase + hs * 512, 512)
                              cols = slice(hs * 512, (hs + 1) * 512)
                              for n in range(NS):
                                  pa = psA[(n + hs) % 2]; pk = f"psA{(n + hs) % 2}"
                                  for kc in range(KC):
                                      mm(V(pa[:TP, :], pk), V(hT[:, kc, n * TP:(n + 1) * TP], "hT"), V(t[:, kc, :], k), start=(kc == 0), stop=(kc == KC - 1))
                                  if kind in ("za", "zb"):
                                      src = OAn if kind == "za" else OBn
                                      dst = OA if kind == "za" else OB
                                      nm = "OAn" if kind == "za" else "OBn"
                                      zt = V(tmpA[:TP, cols], ("tmpA", hs))
                                      act(zt, V(pa[:TP, :], pk), AF.Silu)
                                      tt("dve", V(dst[:TP, n, cols], (kind, n, hs)), zt, V(src[:TP, n, cols], [(nm, n, hh) for hh in range(hs * 4, hs * 4 + 4)]), ALU.mult)
                                  else:
                                      dst = SGA if kind == "ga" else SGB
                                      act(V(dst[:TP, n, cols], (kind, n, hs)), V(pa[:TP, :], pk), AF.Sigmoid)

                      stop_at(7)
                      for n in range(NS):
                          for (src, sname, dst, dname) in ((OA, "za", OAT, "QT"), (OB, "zb", OBT, "KTt")):
                              for g in range(2):
                                  for j in range(4):
                                      kc = 4 * g + j
                                      tr(V(psT[:, g * 512 + j * 128:g * 512 + j * 128 + TP], ("psT", g)), V(src[:TP, n, kc * 128:(kc + 1) * 128], (sname, n, g)), V(identb[:TP, :TP], "identb"))
                                  cp(evac_eng(), V(dst[:, 4 * g:4 * g + 4, n * TP:(n + 1) * TP], dname),
                                     V(psT[:, g * 512:(g + 1) * 512].rearrange("p (j t) -> p j t", t=128)[:, :, :TP], ("psT", g)))
                      for hs in range(2):
                          cols = slice(hs * 512, (hs + 1) * 512)
                          ta_, ka_ = load_slab(wpa[l], hs * 512, 512)
                          tb_, kb_ = load_slab(wpb[l], hs * 512, 512)
                          for n in range(NS):
                              for kc in range(KC):
                                  mm(V(psA[0][:TP, :], "psA0"), V(OAT[:, kc, n * TP:(n + 1) * TP], "QT"), V(ta_[:, kc, :], ka_), start=(kc == 0), stop=(kc == KC - 1))
                              for kc in range(KC):
                                  mm(V(psA[1][:TP, :], "psA1"), V(OBT[:, kc, n * TP:(n + 1) * TP], "KTt"), V(tb_[:, kc, :], kb_), start=(kc == 0), stop=(kc == KC - 1))
                              tt("dve", V(MG[:TP, n, cols], KO("OAn", n, hs)), V(psA[0][:TP, :], "psA0"), V(SGA[:TP, n, cols], ("ga", n, hs)), ALU.mult)
                              tt("dve", V(tmpA[:TP, cols], ("tmpA", hs)), V(psA[1][:TP, :], "psA1"), V(SGB[:TP, n, cols], ("gb", n, hs)), ALU.mult)
                              tt("pool", V(MGb[:TP, n, cols], ("MGb", n, hs)), V(MG[:TP, n, cols], KO("OAn", n, hs)), V(tmpA[:TP, cols], ("tmpA", hs)), ALU.add)
                      MT = hT
                      for n in range(NS):
                          for g in range(2):
                              for j in range(4):
                                  kc = 4 * g + j
                                  tr(V(psT[:, g * 512 + j * 128:g * 512 + j * 128 + TP], ("psT", g)), V(MGb[:TP, n, kc * 128:(kc + 1) * 128], ("MGb", n, g)), V(identb[:TP, :TP], "identb"))
                              cp(evac_eng(), V(MT[:, 4 * g:4 * g + 4, n * TP:(n + 1) * TP], "hT"),
                                 V(psT[:, g * 512:(g + 1) * 512].rearrange("p (j t) -> p j t", t=128)[:, :, :TP], ("psT", g)))
                      for hs in range(2):
                          cols = slice(hs * 512, (hs + 1) * 512)
                          to_, ko_ = load_slab(wo[l], hs * 512, 512)
                          for n in range(NS):
                              pa = psA[(n + hs) % 2]; pk = f"psA{(n + hs) % 2}"
                              for kc in range(KC):
                                  mm(V(pa[:TP, :], pk), V(MT[:, kc, n * TP:(n + 1) * TP], "hT"), V(to_[:, kc, :], ko_), start=(kc == 0), stop=(kc == KC - 1))
                              tt("dve", V(MG[:TP, n, cols], KO("OAn", n, hs)), V(pa[:TP, :], pk), V(modB[:TP, 2 * D + hs * 512:2 * D + (hs + 1) * 512], "modB"), ALU.mult)
                              tt("pool", V(MG[:TP, n, cols], KO("OAn", n, hs)), V(MG[:TP, n, cols], KO("OAn", n, hs)), V(X[:TP, n, cols], "X"), ALU.add)
                      mgk = [k_ for n_ in range(NS) for k_ in KO("OAn", n_)]
                      if not last_layer:
                          P.dma("pool", V(x_dst[tok0:tok0 + TT, :].rearrange("(n p) d -> p n d", p=TP), "D:xres" + sq["name"]), V(MG[:TP, :NS, :], mgk))
                      else:
                          for n in range(NS):
                              ttr(V(tmpA[:TP, :], TA), V(MG[:TP, n, :], mgk), V(MG[:TP, n, :], mgk), V(sm[:TP, n:n + 1], "sm_ss"))
                          ts("dve", V(sm[:TP, 0:NS], "sm_ss"), V(sm[:TP, 0:NS], "sm_ss"), 1.0 / D, ALU.mult, EPS, ALU.add)
                          act(V(sm[:TP, 0:NS], "sm_ss"), V(sm[:TP, 0:NS], "sm_ss"), AF.Sqrt)
                          recip(V(sm[:TP, 2:2 + NS], "sm_rs"), V(sm[:TP, 0:NS], "sm_ss"))
                          for n in range(NS):
                              stt(V(MG[:TP, n, :], mgk), V(MG[:TP, n, :], mgk), V(sm[:TP, 2 + n:3 + n], "sm_rs"), V(fnwB[:TP, :], "fnwB"), ALU.mult, ALU.mult)
                          P.dma("pool", V(sq["y"][tok0:tok0 + TT, :].rearrange("(n p) d -> p n d", p=TP), "y_out"), V(MG[:TP, :NS, :], mgk))

        except _Stop:
            pass
        P.finish()
        print("bass ops:", P.nops)
    return nc


_CACHE = {}


def _consts(T, P):
    ident = np.eye(128, dtype=np.float32)
    idx = np.arange(128)
    same = (idx[:, None] // 64) == (idx[None, :] // 64)
    cmm = (same & (idx[:, None] <= idx[None, :])).astype(np.float32)
    clm = same.astype(np.float32)
    sum_ = (same & (idx[:, None] < idx[None, :])).astype(np.float32)
    half = 32
    inv = (10000.0 ** (-np.arange(half, dtype=np.float32) / half)).astype(np.float32)

    def cs(pos):
        ang = pos.astype(np.float32)[:, None] * inv[None, :]
        return np.cos(ang).astype(np.float32), np.sin(ang).astype(np.float32)

    cp_, sp_ = cs(np.arange(T))
    cs_, ss_ = cs(P + np.arange(32))
    return dict(c_ident=ident, c_cm=cmm, c_cl=clm, c_su=sum_, cos_p=cp_, sin_p=sp_, cos_s=cs_, sin_s=ss_)


def run(inputs, n_cores=8):
    x_prompt = np.asarray(inputs["x_prompt"]); x_sample = np.asarray(inputs["x_sample"])
    B, T, _ = x_prompt.shape
    DB = x_sample.shape[0]
    depth = inputs["w_in"].shape[0]
    PAST = inputs["cache_k"].shape[3]
    ns = DB // n_cores
    cfg = dict(depth=depth, T=T, P=PAST, ns=ns)
    key = (depth, T, PAST, ns)
    if key not in _CACHE:
        _CACHE[key] = build(cfg)
    nc = _CACHE[key]
    consts = _consts(T, PAST)
    f = lambda a: np.ascontiguousarray(np.asarray(a, dtype=np.float32))
    shared = {k: f(inputs[k]) for k in ("norm_w", "w_ada", "b_ada", "w_in", "conv_w", "a_log", "dt_bias", "gdn_norm_w", "diff_norm_w", "w_proj_a", "w_proj_b", "w_out")}
    shared["lam_qk"] = f(inputs["lam_qk"]).reshape(depth, 256)
    shared["final_norm_w"] = f(inputs["final_norm_w"]).reshape(1, D)
    shared.update(consts)
    in_maps = []
    for c in range(n_cores):
        b = c % B
        sl = slice(c * ns, (c + 1) * ns)
        m = dict(shared)
        m["xp"] = f(x_prompt[b])
        m["xs"] = f(x_sample[sl])
        m["c_all"] = f(np.concatenate([np.asarray(inputs["c_prompt"])[b:b + 1], np.asarray(inputs["c_sample"])[sl]], axis=0))
        m["cache_k"] = f(np.asarray(inputs["cache_k"])[:, sl])
        m["cache_v"] = f(np.asarray(inputs["cache_v"])[:, sl])
        m["state_ssm"] = f(np.asarray(inputs["state_ssm"])[:, sl])
        m["state_conv"] = f(np.asarray(inputs["state_conv"])[:, sl])
        in_maps.append(m)
    res = run_bass_kernel_spmd(nc, in_maps, core_ids=list(range(n_cores)))
    R = res.results
    y_prompt = np.stack([R[b]["yp"] for b in range(B)], 0)
    y_sample = np.concatenate([R[c]["ys"] for c in range(n_cores)], 0)
    k_prompt = np.stack([R[b]["kp"] for b in range(B)], 1)
    v_prompt = np.stack([R[b]["vp"] for b in range(B)], 1)
    ssm_prompt = np.stack([R[b]["ssmp"] for b in range(B)], 1)
    conv_prompt = np.stack([R[b]["convp"] for b in range(B)], 1)
    k_sample = np.concatenate([R[c]["ks"] for c in range(n_cores)], 1)
    v_sample = np.concatenate([R[c]["vs"] for c in range(n_cores)], 1)
    ssm_sample = np.concatenate([R[c]["ssms"] for c in range(n_cores)], 1)
    conv_sample = np.concatenate([R[c]["convs"] for c in range(n_cores)], 1)
    outs = (y_prompt, y_sample, k_prompt, v_prompt, ssm_prompt, conv_prompt, k_sample, v_sample, ssm_sample, conv_sample)
    return tuple(np.ascontiguousarray(o, dtype=np.float32) for o in outs)


def kernel(**inputs):
    return run(inputs, n_cores=8)
```
